# Optimizing a Trainium2 kernel written in Bass

```python
import jax, jax.numpy as jnp
from jax import lax
import numpy as np

D_MODEL = 2048
BATCH = 2
SEQ = 8192
DEPTH = 2

HEAD_DIM = 128
Q_BLOCK = 128
GRID_W = 64
ROPE_THETA = 10000.0
NORM_EPS = 1e-6
NEG_INF = -1e30

A_HEADS = 8
DILATED_BRANCHES = ((128, 1), (512, 4), (2048, 16))

B_Q_HEADS = 8
B_KV_HEADS = 2
B_GROUP = B_Q_HEADS // B_KV_HEADS

C_HEADS = 16
C_Q_RANK = 512
C_KV_RANK = 512
C_NOPE_DIM = 128
C_ROPE_DIM = 64
C_V_DIM = 128

D_FF = 5632
CONV_WIDTH = 3

A_QKV_COLS = 3 * A_HEADS * HEAD_DIM
B_Q_COLS = B_Q_HEADS * HEAD_DIM
B_KV_COLS = B_KV_HEADS * HEAD_DIM
IN0_COLS = A_QKV_COLS + B_Q_COLS + 2 * B_KV_COLS
MIX0_WIDTH = (A_HEADS + B_Q_HEADS) * HEAD_DIM
IN1_COLS = C_Q_RANK + C_KV_RANK + C_ROPE_DIM
MIX1_WIDTH = C_HEADS * C_V_DIM

kernel_name = 'hybrid_dilated_gqa_mla_convglu_encoder'


def rms_norm(x, g):
    xf = x.astype(jnp.float32)
    y = xf * lax.rsqrt(jnp.mean(xf * xf, axis=-1, keepdims=True) + NORM_EPS)
    return (y * g.astype(jnp.float32)).astype(x.dtype)


def rope_cos_sin(pos, dim):
    inv = ROPE_THETA ** (-jnp.arange(0, dim, 2, dtype=jnp.float32) / dim)
    ang = pos.astype(jnp.float32)[:, None] * inv[None, :]
    return jnp.cos(ang), jnp.sin(ang)


def apply_rope(x, cos, sin):
    xf = x.astype(jnp.float32)
    half = x.shape[-1] // 2
    x1, x2 = xf[..., :half], xf[..., half:]
    c = cos[None, :, None, :]
    s = sin[None, :, None, :]
    return jnp.concatenate([x1 * c - x2 * s, x2 * c + x1 * s], axis=-1).astype(x.dtype)


def axial_rope(x, row, col):
    half = x.shape[-1] // 2
    xr = apply_rope(x[..., :half], *rope_cos_sin(row, half))
    xc = apply_rope(x[..., half:], *rope_cos_sin(col, half))
    return jnp.concatenate([xr, xc], axis=-1)


def dense_block_attention(q, k, v, scale):
    bsz, seq, hk, grp, dq = q.shape
    dv = v.shape[-1]
    nb = seq // Q_BLOCK
    qb = jnp.swapaxes(q.reshape(bsz, nb, Q_BLOCK, hk, grp, dq), 0, 1)

    def one_block(qi):
        s = jnp.einsum('bqhgd,bkhd->bhgqk', qi, k, preferred_element_type=jnp.float32) * scale
        p = jax.nn.softmax(s, axis=-1)
        return jnp.einsum('bhgqk,bkhd->bqhgd', p.astype(v.dtype), v)

    out = lax.map(one_block, qb)
    return jnp.swapaxes(out, 0, 1).reshape(bsz, seq, hk * grp * dv)


def dilated_attention(q, k, v, scale):
    bsz, seq, heads, hd = q.shape
    nb = seq // Q_BLOCK
    qb = jnp.swapaxes(q.reshape(bsz, nb, Q_BLOCK, heads, hd), 0, 1)
    starts = jnp.arange(nb, dtype=jnp.int32) * Q_BLOCK

    def one_block(args):
        qi, s0 = args
        qpos = s0 + jnp.arange(Q_BLOCK, dtype=jnp.int32)
        outs, lses = [], []
        for (window, dil) in DILATED_BRANCHES:
            reach = window // (2 * dil)
            offs = dil * jnp.arange(-reach, reach + 1, dtype=jnp.int32)
            kpos = qpos[:, None] + offs[None, :]
            valid = (kpos >= 0) & (kpos < seq)
            idx = jnp.clip(kpos, 0, seq - 1)
            kg = k[:, idx]
            vg = v[:, idx]
            s = jnp.einsum('bqhd,bqjhd->bhqj', qi, kg, preferred_element_type=jnp.float32) * scale
            s = jnp.where(valid[None, None], s, NEG_INF)
            m = jnp.max(s, axis=-1, keepdims=True)
            e = jnp.exp(s - m)
            den = jnp.sum(e, axis=-1)
            o = jnp.einsum('bhqj,bqjhd->bqhd', (e / den[..., None]).astype(vg.dtype), vg,
                           preferred_element_type=jnp.float32)
            outs.append(o)
            lses.append(m[..., 0] + jnp.log(den))
        alpha = jax.nn.softmax(jnp.stack(lses, axis=0), axis=0)
        alpha = jnp.transpose(alpha, (0, 1, 3, 2))[..., None]
        return jnp.sum(alpha * jnp.stack(outs, axis=0), axis=0).astype(q.dtype)

    out = lax.map(one_block, (qb, starts))
    return jnp.swapaxes(out, 0, 1).reshape(bsz, seq, heads, hd)


def mixer_dilated_gqa(h, w_in, q_norm, k_norm, w_out, pos, row, col):
    bsz, seq, _ = h.shape
    proj = h @ w_in
    c1 = A_QKV_COLS
    c2 = c1 + B_Q_COLS
    c3 = c2 + B_KV_COLS
    a_qkv = proj[..., :c1].reshape(bsz, seq, 3, A_HEADS, HEAD_DIM)
    b_q = proj[..., c1:c2].reshape(bsz, seq, B_Q_HEADS, HEAD_DIM)
    b_k = proj[..., c2:c3].reshape(bsz, seq, B_KV_HEADS, HEAD_DIM)
    b_v = proj[..., c3:].reshape(bsz, seq, B_KV_HEADS, HEAD_DIM)
    scale = HEAD_DIM ** -0.5

    cos, sin = rope_cos_sin(pos, HEAD_DIM)
    a_q = apply_rope(a_qkv[:, :, 0], cos, sin)
    a_k = apply_rope(a_qkv[:, :, 1], cos, sin)
    o_a = dilated_attention(a_q, a_k, a_qkv[:, :, 2], scale).reshape(bsz, seq, A_HEADS * HEAD_DIM)

    b_q = axial_rope(rms_norm(b_q, q_norm), row, col)
    b_k = axial_rope(rms_norm(b_k, k_norm), row, col)
    b_q = b_q.reshape(bsz, seq, B_KV_HEADS, B_GROUP, HEAD_DIM)
    o_b = dense_block_attention(b_q, b_k, b_v, scale)

    return jnp.concatenate([o_a, o_b], axis=-1) @ w_out


def mixer_mla(h, w_in, q_a_norm, kv_a_norm, w_uq, w_ukv, w_out, pos):
    bsz, seq, _ = h.shape
    c = h @ w_in
    c_q = c[..., :C_Q_RANK]
    c_kv = c[..., C_Q_RANK:C_Q_RANK + C_KV_RANK]
    k_r = c[..., C_Q_RANK + C_KV_RANK:]
    cos, sin = rope_cos_sin(pos, C_ROPE_DIM)

    q = (rms_norm(c_q, q_a_norm) @ w_uq).reshape(bsz, seq, C_HEADS, C_NOPE_DIM + C_ROPE_DIM)
    q = jnp.concatenate([q[..., :C_NOPE_DIM], apply_rope(q[..., C_NOPE_DIM:], cos, sin)], axis=-1)

    kv = (rms_norm(c_kv, kv_a_norm) @ w_ukv).reshape(bsz, seq, C_HEADS, C_NOPE_DIM + C_V_DIM)
    k_nope, v = kv[..., :C_NOPE_DIM], kv[..., C_NOPE_DIM:]
    k_rope = apply_rope(k_r[:, :, None, :], cos, sin)
    k = jnp.concatenate([k_nope, jnp.broadcast_to(k_rope, (bsz, seq, C_HEADS, C_ROPE_DIM))], axis=-1)

    o = dense_block_attention(q[:, :, :, None, :], k, v, (C_NOPE_DIM + C_ROPE_DIM) ** -0.5)
    return o @ w_out


def depthwise_conv3(h, w, b):
    ch = h.shape[-1]
    y = lax.conv_general_dilated(h, w[:, None, :].astype(h.dtype), window_strides=(1,),
                                 padding=((CONV_WIDTH // 2, CONV_WIDTH // 2),),
                                 dimension_numbers=('NWC', 'WIO', 'NWC'),
                                 feature_group_count=ch)
    return y + b.astype(h.dtype)


def conv_glu(h, w_up, conv_w, conv_b, w_down):
    u = h @ w_up
    gate, val = u[..., :D_FF], u[..., D_FF:]
    gate = depthwise_conv3(gate, conv_w, conv_b)
    return (jax.nn.gelu(gate, approximate=False) * val) @ w_down


def setup_inputs(seed: int = 0) -> dict:
    key = jax.random.key(seed)
    ks = jax.random.split(key, 32)
    f32 = jnp.float32

    def w(k, shape, fan_in):
        return jax.random.normal(k, shape, f32) * fan_in ** -0.5

    def gain(k, n):
        return 1.0 + 0.02 * jax.random.normal(k, (n,), f32)

    return {
        'x': jax.random.normal(ks[0], (BATCH, SEQ, D_MODEL), f32),
        'l0_mix_pre': gain(ks[1], D_MODEL),
        'l0_w_in': w(ks[2], (D_MODEL, IN0_COLS), D_MODEL),
        'l0_q_norm': gain(ks[3], HEAD_DIM),
        'l0_k_norm': gain(ks[4], HEAD_DIM),
        'l0_w_out': w(ks[5], (MIX0_WIDTH, D_MODEL), MIX0_WIDTH),
        'l0_mix_post': gain(ks[6], D_MODEL),
        'l0_ffn_pre': gain(ks[7], D_MODEL),
        'l0_w_up': w(ks[8], (D_MODEL, 2 * D_FF), D_MODEL),
        'l0_conv_w': w(ks[9], (CONV_WIDTH, D_FF), CONV_WIDTH),
        'l0_conv_b': 0.02 * jax.random.normal(ks[10], (D_FF,), f32),
        'l0_w_down': w(ks[11], (D_FF, D_MODEL), D_FF),
        'l0_ffn_post': gain(ks[12], D_MODEL),
        'l1_mix_pre': gain(ks[13], D_MODEL),
        'l1_w_in': w(ks[14], (D_MODEL, IN1_COLS), D_MODEL),
        'l1_q_a_norm': gain(ks[15], C_Q_RANK),
        'l1_kv_a_norm': gain(ks[16], C_KV_RANK),
        'l1_w_uq': w(ks[17], (C_Q_RANK, C_HEADS * (C_NOPE_DIM + C_ROPE_DIM)), C_Q_RANK),
        'l1_w_ukv': w(ks[18], (C_KV_RANK, C_HEADS * (C_NOPE_DIM + C_V_DIM)), C_KV_RANK),
        'l1_w_out': w(ks[19], (MIX1_WIDTH, D_MODEL), MIX1_WIDTH),
        'l1_mix_post': gain(ks[20], D_MODEL),
        'l1_ffn_pre': gain(ks[21], D_MODEL),
        'l1_w_up': w(ks[22], (D_MODEL, 2 * D_FF), D_MODEL),
        'l1_conv_w': w(ks[23], (CONV_WIDTH, D_FF), CONV_WIDTH),
        'l1_conv_b': 0.02 * jax.random.normal(ks[24], (D_FF,), f32),
        'l1_w_down': w(ks[25], (D_FF, D_MODEL), D_FF),
        'l1_ffn_post': gain(ks[26], D_MODEL),
    }


def reference(x,
              l0_mix_pre, l0_w_in, l0_q_norm, l0_k_norm, l0_w_out, l0_mix_post,
              l0_ffn_pre, l0_w_up, l0_conv_w, l0_conv_b, l0_w_down, l0_ffn_post,
              l1_mix_pre, l1_w_in, l1_q_a_norm, l1_kv_a_norm, l1_w_uq, l1_w_ukv, l1_w_out,
              l1_mix_post, l1_ffn_pre, l1_w_up, l1_conv_w, l1_conv_b, l1_w_down, l1_ffn_post):
    bsz, seq, _ = x.shape
    rows = seq // GRID_W
    pos = jnp.arange(seq, dtype=jnp.int32)
    row = jnp.repeat(jnp.arange(rows, dtype=jnp.int32), GRID_W)
    col = jnp.tile(jnp.arange(GRID_W, dtype=jnp.int32), rows)

    layers = (
        dict(mix_pre=l0_mix_pre, mixer=(l0_w_in, l0_q_norm, l0_k_norm, l0_w_out), mix_post=l0_mix_post,
             ffn_pre=l0_ffn_pre, ffn=(l0_w_up, l0_conv_w, l0_conv_b, l0_w_down), ffn_post=l0_ffn_post),
        dict(mix_pre=l1_mix_pre, mixer=(l1_w_in, l1_q_a_norm, l1_kv_a_norm, l1_w_uq, l1_w_ukv, l1_w_out),
             mix_post=l1_mix_post, ffn_pre=l1_ffn_pre, ffn=(l1_w_up, l1_conv_w, l1_conv_b, l1_w_down),
             ffn_post=l1_ffn_post),
    )

    for i in range(DEPTH):
        p = layers[i]
        h = rms_norm(x, p['mix_pre'])
        if i % 2 == 0:
            m = mixer_dilated_gqa(h, *p['mixer'], pos, row, col)
        else:
            m = mixer_mla(h, *p['mixer'], pos)
        x = x + rms_norm(m, p['mix_post'])
        h = rms_norm(x, p['ffn_pre'])
        x = x + rms_norm(conv_glu(h, *p['ffn']), p['ffn_post'])
    return x
```

```python
import contextlib
import os
import numpy as np
import ml_dtypes
import concourse.bass as bass
import concourse.mybir as mybir
from concourse.bass_utils import run_bass_kernel_spmd

F32 = mybir.dt.float32
BF16 = mybir.dt.bfloat16
ALU = mybir.AluOpType
AF = mybir.ActivationFunctionType
AX = mybir.AxisListType
NPBF = ml_dtypes.bfloat16

D = 2048
KC = 16
DFF = 5632
NFF = 44
EPS = 1e-6
HALO = 1024
G = 512


class Buf:
    def __init__(self, t, name):
        self.t = t
        self.name = name
        self.writers = []
        self.readers = []
        self.gen_deps = set()
        self.sem = None
        self.sem_cnt = 0
        self.sem_hist = []
        self.is_psum = False

    def __getitem__(self, idx):
        return self.t[idx]

    def ap(self):
        return self.t.ap() if hasattr(self.t, "ap") else self.t[:]


class Ins:
    __slots__ = ("eng", "fn", "deps", "is_dma", "dsem", "dcnt", "sig", "idx", "bar")


class Prog:
    ENGS = ("pe", "act", "dve", "pool", "sp")

    def __init__(self, nc, stack):
        self.nc = nc
        self.stack = stack
        self.ins = []
        self.n_dma_sems = 0
        self.all_bufs = []
        self.barriers = []
        self.cur_bar = -1

    def sbuf(self, name, shape, dt, stack=None):
        st = stack or self.stack
        b = Buf(st.enter_context(self.nc.sbuf_tensor(name, list(shape), dt)), name)
        self.all_bufs.append(b)
        return b

    def psum(self, name, shape, dt=F32):
        b = Buf(self.stack.enter_context(self.nc.psum_tensor(name, list(shape), dt)), name)
        b.is_psum = True
        self.all_bufs.append(b)
        return b

    def dram(self, name, shape, dt, kind="Internal"):
        b = Buf(self.nc.dram_tensor(name, list(shape), dt, kind=kind), name)
        self.all_bufs.append(b)
        return b

    def _compress(self, lst):
        last = {}
        out = []
        for i in lst:
            it = self.ins[i]
            if it.is_dma:
                out.append(i)
            else:
                last[it.eng] = i
        out.extend(last.values())
        return out

    def barrier(self):
        last = {}
        for it in self.ins:
            if not it.is_dma:
                last[it.eng] = it.idx
        snap = []
        for b in self.all_bufs:
            for sm, c in b.sem_hist:
                if c > 0:
                    snap.append((sm, c))
        self.barriers.append((set(last.values()), snap))
        self.cur_bar = len(self.barriers) - 1

    def op(self, eng, fn, reads=(), writes=(), pwrites=(), dma=False):
        i = len(self.ins)
        it = Ins()
        it.eng = eng; it.fn = fn; it.is_dma = dma; it.idx = i
        it.dsem = None; it.dcnt = 0; it.sig = None; it.bar = self.cur_bar
        deps = set()
        xr = [b for b in reads if b.is_psum]
        if xr:
            reads = [b for b in reads if not b.is_psum]
            writes = list(writes) + xr
        for b in reads:
            deps.update(b.writers)
        for b in writes:
            b.gen_deps = set(b.readers) | set(b.writers)
            deps.update(b.gen_deps)
        for b in pwrites:
            if b.readers or not b.writers:
                b.gen_deps = set(b.readers) | set(b.writers)
                b.writers = []
                b.readers = []
            deps.update(b.gen_deps)
        it.deps = deps
        self.ins.append(it)
        for b in reads:
            b.readers.append(i)
            if len(b.readers) > 4:
                b.readers = self._compress(b.readers)
        for b in writes:
            b.writers = [i]
            b.readers = []
        for b in pwrites:
            b.writers.append(i)
            if len(b.writers) > 4:
                b.writers = self._compress(b.writers)
        if dma:
            tgt = (list(writes) + list(pwrites))[0]
            if tgt.sem is None or tgt.sem_cnt >= 16000:
                tgt.sem = self.stack.enter_context(self.nc.semaphore("ds%d" % self.n_dma_sems))
                tgt.sem_cnt = 0
                tgt.sem_hist.append([tgt.sem, 0])
                self.n_dma_sems += 1
            tgt.sem_cnt += 16
            tgt.sem_hist[-1][1] = tgt.sem_cnt
            it.dsem = tgt.sem
            it.dcnt = tgt.sem_cnt
        return i

    def dma(self, q, out_ap, in_ap, reads, writes=(), pwrites=(), **kw):
        return self.op(q, lambda e: e.dma_start(out=out_ap, in_=in_ap, **kw),
                       reads=reads, writes=writes, pwrites=pwrites, dma=True)

    def emit(self, final_wait=()):
        nc = self.nc
        ins = self.ins
        need = [False] * len(ins)
        for it in ins:
            for d in it.deps:
                dd = ins[d]
                if dd.is_dma:
                    continue
                if dd.eng == "pe" and it.eng == "pe" and not it.is_dma:
                    continue
                need[d] = True
        for cdeps, _ in self.barriers:
            for d in cdeps:
                need[d] = True
        cnt = {e: 0 for e in self.ENGS}
        esems = {e: [] for e in self.ENGS}
        for it in ins:
            if not it.is_dma and need[it.idx]:
                cnt[it.eng] += 1
                k = (cnt[it.eng] - 1) // 16000
                if k >= len(esems[it.eng]):
                    esems[it.eng].append(self.stack.enter_context(nc.semaphore("es_%s%d" % (it.eng, k))))
                it.sig = (esems[it.eng][k], cnt[it.eng] - k * 16000, k)
        self.sig_counts = dict(cnt)
        streams = {e: [] for e in self.ENGS}
        for it in ins:
            streams[it.eng].append(it)
        self.n_waits = 0
        final_targets = []
        for b in final_wait:
            for sm, c in b.sem_hist:
                final_targets.append((sm, c))

        def run_stream(ename, eng):
            waited = {}
            bar_done = -1

            def do_wait(key, s, v):
                if waited.get(key, 0) >= v:
                    return
                eng.wait_ge(s, v)
                waited[key] = v
                self.n_waits += 1

            for it in streams[ename]:
                while bar_done < it.bar:
                    bar_done += 1
                    cdeps, snap = self.barriers[bar_done]
                    for d in cdeps:
                        dd = ins[d]
                        do_wait(("e", dd.eng, dd.sig[2]), dd.sig[0], dd.sig[1])
                    for sm, c in snap:
                        do_wait(("d", id(sm)), sm, c)
                reqs = {}
                for d in it.deps:
                    dd = ins[d]
                    if dd.is_dma:
                        key = ("d", id(dd.dsem))
                        s, v = dd.dsem, dd.dcnt
                    else:
                        if dd.eng == "pe" and ename == "pe" and not it.is_dma:
                            continue
                        key = ("e", dd.eng, dd.sig[2])
                        s, v = dd.sig[0], dd.sig[1]
                    if key not in reqs or reqs[key][1] < v:
                        reqs[key] = (s, v)
                for key, (s, v) in reqs.items():
                    do_wait(key, s, v)
                bi = it.fn(eng)
                if it.is_dma:
                    bi.then_inc(it.dsem, 16)
                elif it.sig:
                    bi.then_inc(it.sig[0], 1)
            if ename == "sp":
                for s, v in final_targets:
                    eng.wait_ge(s, v)

        with nc.Block() as block:
            @block.tensor
            def _(e):
                run_stream("pe", e)

            @block.scalar
            def _(e):
                run_stream("act", e)

            @block.vector
            def _(e):
                run_stream("dve", e)

            @block.gpsimd
            def _(e):
                run_stream("pool", e)

            @block.sync
            def _(e):
                run_stream("sp", e)


class Rot:
    def __init__(self, bufs):
        self.bufs = bufs
        self.i = 0

    def next(self):
        b = self.bufs[self.i % len(self.bufs)]
        self.i += 1
        return b


class KB:
    def __init__(self, S):
        self.S = S
        self.TC = S // 4
        self.NG = self.TC // G
        self.TCX = self.TC + 2 * HALO
        self.NT = S // 128
        self.nc = bass.Bass("TRN2", target_bir_lowering=False)
        self.stack = contextlib.ExitStack()
        self.P = Prog(self.nc, self.stack)
        self.ext_in = {}
        self.ext_out = {}

    def din(self, name, shape, dt=F32):
        b = self.P.dram(name, shape, dt, kind="ExternalInput")
        self.ext_in[name] = b
        return b

    def dout(self, name, shape, dt=F32):
        b = self.P.dram(name, shape, dt, kind="ExternalOutput")
        self.ext_out[name] = b
        return b

    def common(self):
        P = self.P
        self.ps = [P.psum("ps%d" % i, [128, 512], F32) for i in range(8)]
        self.identf = P.sbuf("identf", [128, 128], F32)
        self.ones = P.sbuf("ones", [128, 128], BF16)
        self.epsb = P.sbuf("epsb", [128, 1], F32)
        idd = self.din("c_ident", [128, 128], F32)
        P.dma("sp", self.identf[:], idd.ap(), reads=[idd], writes=[self.identf])
        P.op("dve", lambda e: e.memset(self.ones[:], 1.0), writes=[self.ones])
        P.op("dve", lambda e: e.memset(self.epsb[:], EPS), writes=[self.epsb])

    def load_const(self, name, shape, dt=F32, q="sp"):
        d = self.din(name, shape, dt)
        sb = self.P.sbuf("sb_" + name, shape, dt)
        self.P.dma(q, sb[:], d.ap(), reads=[d], writes=[sb])
        return sb


def mm(P, ps, out_ap, lhsT, rhs, start, stop, reads):
    P.op("pe", lambda e: e.matmul(out_ap, lhsT=lhsT, rhs=rhs, start=start, stop=stop), reads=reads, pwrites=[ps])


def rstd_from_ss(K, ss_ps, rstd, scale):
    P = K.P
    P.op("act", lambda e: e.activation(out=rstd[:], in_=ss_ps[:], func=AF.Sqrt, bias=K.epsb[:], scale=scale),
         reads=[ss_ps, K.epsb], writes=[rstd])
    P.op("dve", lambda e: e.reciprocal(rstd[:], rstd[:]), reads=[rstd], writes=[rstd])


def norm_group(K, xT_g, gain, hT, hcol0, psr, sq, rstd):
    P = K.P
    import os
    NF = int(os.environ.get("S1_NORM", "15"))
    if NF & 1:
        P.op("act", lambda e: e.activation(out=sq[:], in_=xT_g[:], func=AF.Square), reads=[xT_g], writes=[sq])
    ss = psr.next()
    if os.environ.get("S1_SKIPB"):
        ss = psr.next()
    if NF & 2:
      for c in range(KC):
        mm(P, ss, ss[:], K.ones[:], sq[:, c, :], c == 0, c == KC - 1, [K.ones, sq])
    if NF & 4:
        rstd_from_ss(K, ss, rstd, 1.0 / D)
    if not (NF & 8):
        return
    for c in range(KC):
        P.op("dve", lambda e, c=c: e.scalar_tensor_tensor(out=hT[:, c, hcol0:hcol0 + G], in0=xT_g[:, c, :],
                                                           scalar=gain[:, c:c + 1], in1=rstd[:],
                                                           op0=ALU.mult, op1=ALU.mult),
             reads=[xT_g, gain, rstd], pwrites=[hT])


def transpose_in_group(K, src, row0, xT_g, xblks, psr, cnt):
    P = K.P
    for tb in range(4):
        xb = xblks.next()
        P.dma("sp", xb[:], src.ap()[row0 + tb * 128: row0 + (tb + 1) * 128, :], reads=[src], writes=[xb])
        for cq in range(4):
            ps = psr.next()
            for c4 in range(4):
                c = cq * 4 + c4
                P.op("pe", lambda e, ps=ps, xb=xb, c=c, c4=c4: e.transpose(ps[:, c4 * 128:(c4 + 1) * 128],
                                                                             xb[:, c * 128:(c + 1) * 128], K.identf[:]),
                     reads=[xb, K.identf], pwrites=[ps])
            eng = "act" if (cnt[0] % 2 == 0) else "dve"
            cnt[0] += 1
            outap = xT_g[:, cq * 4:(cq + 1) * 4, tb * 128:(tb + 1) * 128]
            inap = ps[:].rearrange("p (c t) -> p c t", c=4)
            if eng == "act":
                P.op("act", lambda e, o=outap, i=inap: e.activation(out=o, in_=i, func=AF.Copy), reads=[ps], pwrites=[xT_g])
            else:
                P.op("dve", lambda e, o=outap, i=inap: e.tensor_copy(o, i), reads=[ps], pwrites=[xT_g])


def rope_apply(K, src_ap, src_buf, rows, cos_ap, sin_ap, tabs, perm, out_ap, out_buf, psr, tb_bf, tb_f):
    P = K.P
    qb = tb_bf.next()
    P.op("act", lambda e: e.activation(out=qb[:rows, :], in_=src_ap, func=AF.Copy), reads=[src_buf], writes=[qb])
    ps2 = psr.next()
    mm(P, ps2, ps2[:rows, :], perm[:rows, :rows], qb[:rows, :], True, True, [perm, qb])
    t1 = tb_f.next()
    P.op("dve", lambda e: e.tensor_tensor(out=t1[:rows, :], in0=src_ap, in1=cos_ap, op=ALU.mult),
         reads=[src_buf] + tabs, writes=[t1])
    t2 = tb_f.next()
    P.op("dve", lambda e: e.tensor_tensor(out=t2[:rows, :], in0=ps2[:rows, :], in1=sin_ap, op=ALU.mult),
         reads=[ps2] + tabs, writes=[t2])
    P.op("pool", lambda e: e.tensor_tensor(out=out_ap, in0=t1[:rows, :], in1=t2[:rows, :], op=ALU.add),
         reads=[t1, t2], pwrites=[out_buf])


def load_w(K, wpool, wd, tile_idx, kc, m):
    wt = wpool.next()
    if os.environ.get("W_SWDGE"):
        K.P.dma("pool", wt[:, :kc, :m], wd.ap()[tile_idx], reads=[wd], writes=[wt])
        return wt
    stg = K.wstage.next()
    sv = stg[:, 0:kc * m].rearrange("p (k m) -> p k m", m=m)
    K.P.dma("sp", sv, wd.ap()[tile_idx], reads=[wd], writes=[stg])
    K.P.op("pool", lambda e: e.tensor_copy(wt[:, :kc, :m], sv), reads=[stg], writes=[wt])
    return wt


def attention_block(K, q_parts, k_parts, v_fn, den_fn, ntiles, mask_fn, scale, out_ap, out_buf,
                    ps_s, ps_o, ps_d, pbufs, rdbuf, extra_reads):
    P = K.P
    psO = ps_o.next()
    psD = ps_d.next()
    pend = []

    def emit_s(i):
        psS = ps_s.next()
        kp = k_parts(i)
        for j, (ka, qa) in enumerate(zip(kp, q_parts)):
            mm(P, psS, psS[:], ka, qa, j == 0, j == len(kp) - 1, extra_reads)
        pb = pbufs.next()
        P.op("act", lambda e: e.activation(out=pb[:], in_=psS[:], func=AF.Exp, scale=scale), reads=[psS], writes=[pb])
        if mask_fn is not None:
            mk, mkbuf = mask_fn(i)
            P.op("dve", lambda e: e.tensor_tensor(out=pb[:], in0=pb[:], in1=mk, op=ALU.mult), reads=[pb, mkbuf], writes=[pb])
        return pb

    def emit_o(i, pb):
        va, vbufs = v_fn(i)
        mm(P, psO, psO[:], va, pb[:], i == 0, i == ntiles - 1, [pb] + vbufs)
        da, dbufs = den_fn(i)
        mm(P, psD, psD[:], da, pb[:], i == 0, i == ntiles - 1, [pb] + dbufs)

    LAG = 2
    for i in range(ntiles):
        pend.append((i, emit_s(i)))
        if len(pend) > LAG:
            emit_o(*pend.pop(0))
    while pend:
        emit_o(*pend.pop(0))
    rd = rdbuf.next()
    P.op("dve", lambda e: e.reciprocal(rd[:], psD[:]), reads=[psD], writes=[rd])
    P.op("dve", lambda e: e.tensor_tensor(out=out_ap, in0=psO[:], in1=rd[:], op=ALU.mult), reads=[psO, rd], pwrites=[out_buf])


def stage1(K, T):
    P = K.P
    TC, NG, TCX = K.TC, K.NG, K.TCX
    NGX = TCX // G
    st = K.stack
    xT_g = P.sbuf("xT_g", [128, KC, G], F32)
    hT_g = P.sbuf("hT_g", [128, KC, G], BF16)
    sq = P.sbuf("sq", [128, KC, G], BF16)
    rstd = P.sbuf("rstd", [128, G], F32)
    xblks = Rot([P.sbuf("xblk%d" % i, [128, D], F32) for i in range(1)])
    wfm = Rot([P.sbuf("wfm%d" % i, [128, KC, 128], BF16) for i in range(3)])
    wv = Rot([P.sbuf("wv%d" % i, [128, KC, 512], BF16) for i in range(1)])
    K.wstage = Rot([P.sbuf("wstg%d" % i, [128, KC * 512], F32) for i in range(1)])
    csA = Rot([P.sbuf("csA%d" % i, [128, 2, G], F32) for i in range(2)])
    csB = Rot([P.sbuf("csB%d" % i, [128, 2, G], F32) for i in range(2)])
    tb_f = Rot([P.sbuf("tbf%d" % i, [128, G], F32) for i in range(5)])
    tb_bf = Rot([P.sbuf("tbb%d" % i, [128, G], BF16) for i in range(3)])
    kst = Rot([P.sbuf("kst%d" % i, [128, 8, G], BF16) for i in range(1)])
    qst = Rot([P.sbuf("qst%d" % i, [128, 16, G], BF16) for i in range(1)])
    kbst = Rot([P.sbuf("kbst%d" % i, [128, 2, G], BF16) for i in range(1)])
    vst = Rot([P.sbuf("vst%d" % i, [128, 4, 1024], BF16) for i in range(1)])
    vbst = Rot([P.sbuf("vbst%d" % i, [128, 4, 256], BF16) for i in range(1)])
    gpre = K.load_const("l0_mix_pre", [128, KC])
    gq = K.load_const("l0_q_norm", [128, 1])
    gk = K.load_const("l0_k_norm", [128, 1])
    permA = K.load_const("c_permA", [128, 128], BF16)
    permB = K.load_const("c_permB", [128, 128], BF16)
    psr = Rot(K.ps)
    cnt = [int(os.environ.get('S1_CNT0', '0'))]
    x_ext, w_fm, w_vA, w_vB = T["x_ext"], T["w0_fm"], T["w0_vA"], T["w0_vB"]
    cosA, sinA, cosB, sinB = T["cosA"], T["sinA"], T["cosB"], T["sinB"]

    def fm_chunk(tile_idx, evac):
        wt = load_w(K, wfm, w_fm, tile_idx, KC, 128)
        ps = psr.next()
        for kc in range(KC):
            mm(P, ps, ps[:], wt[:, kc, :], hT_g[:, kc, :], kc == 0, kc == KC - 1, [wt, hT_g])
        evac(ps)

    STOP = int(os.environ.get("S1_STOP", "99"))
    for j in range(NGX):
        own = 2 <= j < 2 + NG
        go = j - 2
        GS = os.environ.get("S1_GROUPS")
        if GS and str(j) not in GS.split(","):
            continue
        transpose_in_group(K, x_ext, j * G, xT_g, xblks, psr, cnt)
        if own:
            P.dma("sp", T["xT0"].ap()[:, :, go * G:(go + 1) * G].rearrange("c p t -> p c t"), xT_g[:],
                  reads=[xT_g], pwrites=[T["xT0"]])
        if STOP <= 1:
            continue
        norm_group(K, xT_g, gpre, hT_g, 0, psr, sq, rstd)
        if STOP <= 2:
            continue
        ca = csA.next()
        P.dma("sp", ca[:, 0, :], cosA.ap()[:, j * G:(j + 1) * G], reads=[cosA], pwrites=[ca])
        P.dma("sp", ca[:, 1, :], sinA.ap()[:, j * G:(j + 1) * G], reads=[sinA], pwrites=[ca])
        ks = kst.next()
        for h in range(8):
            fm_chunk(8 + h, lambda ps, h=h: rope_apply(K, ps[:], ps, 128, ca[:, 0, :], ca[:, 1, :], [ca], permA,
                                                       ks[:, h, :], ks, psr, tb_bf, tb_f))
        P.dma("sp", T["kTA"].ap()[:, :, j * G:(j + 1) * G].rearrange("h p t -> p h t"), ks[:], reads=[ks], pwrites=[T["kTA"]])
        if STOP <= 3:
            continue
        vs = vst.next()
        for vt in range(2):
            wt = load_w(K, wv, w_vA, vt, KC, 512)
            for tb in range(4):
                ps = psr.next()
                for kc in range(KC):
                    mm(P, ps, ps[:], hT_g[:, kc, tb * 128:(tb + 1) * 128], wt[:, kc, :], kc == 0, kc == KC - 1, [wt, hT_g])
                P.op("act", lambda e, ps=ps, tb=tb, vt=vt: e.activation(out=vs[:, tb, vt * 512:(vt + 1) * 512], in_=ps[:], func=AF.Copy),
                     reads=[ps], pwrites=[vs])
        P.dma("sp", T["VA"].ap()[j * G:(j + 1) * G, :].rearrange("(tb p) c -> p tb c", p=128), vs[:], reads=[vs], pwrites=[T["VA"]])
        if not own or STOP <= 4:
            continue
        cb = csB.next()
        P.dma("sp", cb[:, 0, :], cosB.ap()[:, go * G:(go + 1) * G], reads=[cosB], pwrites=[cb])
        P.dma("sp", cb[:, 1, :], sinB.ap()[:, go * G:(go + 1) * G], reads=[sinB], pwrites=[cb])
        qs = qst.next()
        for h in range(8):
            fm_chunk(h, lambda ps, h=h: rope_apply(K, ps[:], ps, 128, ca[:, 0, :], ca[:, 1, :], [ca], permA,
                                                   qs[:, h, :], qs, psr, tb_bf, tb_f))

        def b_evac(ps, gain, out_ap, out_buf):
            sqb = tb_bf.next()
            P.op("act", lambda e: e.activation(out=sqb[:], in_=ps[:], func=AF.Square), reads=[ps], writes=[sqb])
            ps3 = psr.next()
            mm(P, ps3, ps3[:], K.ones[:], sqb[:], True, True, [K.ones, sqb])
            r = tb_f.next()
            rstd_from_ss(K, ps3, r, 1.0 / 128)
            qn = tb_f.next()
            P.op("dve", lambda e: e.scalar_tensor_tensor(out=qn[:], in0=ps[:], scalar=gain[:, 0:1], in1=r[:],
                                                          op0=ALU.mult, op1=ALU.mult), reads=[ps, gain, r], writes=[qn])
            rope_apply(K, qn[:], qn, 128, cb[:, 0, :], cb[:, 1, :], [cb], permB, out_ap, out_buf, psr, tb_bf, tb_f)

        for h in range(8):
            fm_chunk(16 + h, lambda ps, h=h: b_evac(ps, gq, qs[:, 8 + h, :], qs))
        P.dma("sp", T["qT0"].ap()[:, :, go * G:(go + 1) * G].rearrange("h p t -> p h t"), qs[:], reads=[qs], pwrites=[T["qT0"]])
        kb = kbst.next()
        for h in range(2):
            fm_chunk(24 + h, lambda ps, h=h: b_evac(ps, gk, kb[:, h, :], kb))
        P.dma("sp", T["kTB"].ap()[:, :, go * G:(go + 1) * G].rearrange("h p t -> p h t"), kb[:], reads=[kb], pwrites=[T["kTB"]])
        vb = vbst.next()
        wt = load_w(K, wv, w_vB, 0, KC, 256)
        for tb in range(4):
            ps = psr.next()
            for kc in range(KC):
                mm(P, ps, ps[:, 0:256], hT_g[:, kc, tb * 128:(tb + 1) * 128], wt[:, kc, 0:256], kc == 0, kc == KC - 1, [wt, hT_g])
            P.op("act", lambda e, ps=ps, tb=tb: e.activation(out=vb[:, tb, :], in_=ps[:, 0:256], func=AF.Copy), reads=[ps], pwrites=[vb])
        P.dma("sp", T["VB"].ap()[go * G:(go + 1) * G, :].rearrange("(tb p) c -> p tb c", p=128), vb[:], reads=[vb], pwrites=[T["VB"]])


def fm_tile(w, c0, m):
    kdim = w.shape[0]
    return np.ascontiguousarray(w[:, c0:c0 + m].reshape(kdim // 128, 128, m).transpose(1, 0, 2))


def vec_tile(v):
    return np.ascontiguousarray(v.reshape(-1, 128).T)


def rope_tables(pos, dim, rows_map):
    inv = (np.float32(10000.0) ** (-np.arange(0, dim, 2, dtype=np.float32) / np.float32(dim))).astype(np.float32)
    ang = pos.astype(np.float32)[None, :] * inv[:, None]
    c = np.cos(ang).astype(np.float32)
    s = np.sin(ang).astype(np.float32)
    idx = np.array([r[0] for r in rows_map])
    sg = np.array([r[1] for r in rows_map], dtype=np.float32)[:, None]
    return np.ascontiguousarray(c[idx]), np.ascontiguousarray(s[idx] * sg)


def perm_matrix(n, half):
    m = np.zeros((n, n), dtype=np.float32)
    for d in range(n):
        w = d % (2 * half)
        m[d + half if w < half else d - half, d] = 1.0
    return m.astype(NPBF)


def dil_mult(o):
    o = np.asarray(o)
    m = (np.abs(o) <= 64).astype(np.float32)
    m += ((o % 4 == 0) & (np.abs(o) <= 256))
    m += ((o % 16 == 0) & (np.abs(o) <= 1024))
    return m


def host_consts(S):
    TC = S // 4
    c = {}
    c["c_ident"] = np.eye(128, dtype=np.float32)
    c["c_permA"] = perm_matrix(128, 64)
    c["c_permB"] = perm_matrix(128, 32)
    c["c_permC"] = perm_matrix(128, 32)
    kk = np.arange(128)[:, None]
    qq = np.arange(512)[None, :]
    c["c_masks"] = np.stack([dil_mult((i - 8) * 128 + kk - qq) for i in range(20)]).astype(NPBF)
    return c


def core_consts(S, core):
    TC = S // 4
    t0 = (core % 4) * TC
    d = {}
    rmA = [(r % 64, -1.0 if r < 64 else 1.0) for r in range(128)]
    pos_ext = np.arange(t0 - HALO, t0 + TC + HALO)
    d["cosA"], d["sinA"] = rope_tables(pos_ext, 128, rmA)
    pos = np.arange(t0, t0 + TC)
    rmB = [((r % 64) % 32, -1.0 if (r % 64) < 32 else 1.0) for r in range(128)]
    cr, sr = rope_tables(pos // 64, 64, rmB)
    cc, sc = rope_tables(pos % 64, 64, rmB)
    d["cosB"] = np.concatenate([cr[:64], cc[64:]], axis=0)
    d["sinB"] = np.concatenate([sr[:64], sc[64:]], axis=0)
    rmC = [(r % 32, -1.0 if r < 32 else 1.0) for r in range(64)]
    d["cosC"], d["sinC"] = rope_tables(pos, 64, rmC)
    nt = (TC + 2 * HALO) // 128
    valid = np.array([1.0 if 0 <= (t0 - HALO + t * 128) < S else 0.0 for t in range(nt)], dtype=np.float32)
    d["denA"] = np.ascontiguousarray(np.broadcast_to(valid[None, :, None], (128, nt, 128))).astype(NPBF)
    return d


def prep_l0_weights(inp):
    w = inp["l0_w_in"]
    cols = [h * 128 for h in range(8)] + [1024 + h * 128 for h in range(8)] + \
           [3072 + h * 128 for h in range(8)] + [4096 + h * 128 for h in range(2)]
    o = {}
    o["w0_fm"] = np.stack([fm_tile(w, c0, 128) for c0 in cols])
    o["w0_vA"] = np.stack([fm_tile(w, 2048 + vt * 512, 512) for vt in range(2)])
    o["w0_vB"] = np.stack([fm_tile(w, 4352, 256)])
    o["l0_mix_pre"] = vec_tile(inp["l0_mix_pre"])
    o["l0_q_norm"] = vec_tile(inp["l0_q_norm"])
    o["l0_k_norm"] = vec_tile(inp["l0_k_norm"])
    return o


def x_ext_for(x, S, core):
    TC = S // 4
    b = core // 4
    t0 = (core % 4) * TC
    out = np.zeros((TC + 2 * HALO, D), dtype=np.float32)
    lo, hi = t0 - HALO, t0 + TC + HALO
    a, e = max(lo, 0), min(hi, S)
    out[a - lo:e - lo] = x[b, a:e]
    return out


def run_launch(K, per_core_inputs, out_names):
    K.P.emit(final_wait=[K.ext_out[n] for n in out_names])
    K.stack.close()
    in_maps = []
    for ci in per_core_inputs:
        in_maps.append({n: np.ascontiguousarray(ci[n]) for n in K.ext_in})
    res = run_bass_kernel_spmd(K.nc, in_maps, core_ids=list(range(8)))
    return [{n: r[n] for n in out_names} for r in res.results]


def build_stage1(S):
    K = KB(S)
    K.common()
    TC, TCX = K.TC, K.TCX
    T = {}
    T["x_ext"] = K.din("x_ext", [TCX, D])
    T["w0_fm"] = K.din("w0_fm", [26, 128, KC, 128])
    T["w0_vA"] = K.din("w0_vA", [2, 128, KC, 512])
    T["w0_vB"] = K.din("w0_vB", [1, 128, KC, 256])
    for n in ("cosA", "sinA"):
        T[n] = K.din(n, [128, TCX])
    for n in ("cosB", "sinB"):
        T[n] = K.din(n, [128, TC])
    T["xT0"] = K.dout("xT0", [KC, 128, TC], F32)
    T["qT0"] = K.dout("qT0", [16, 128, TC], BF16)
    T["kTA"] = K.dout("kTA", [8, 128, TCX], BF16)
    T["VA"] = K.dout("VA", [TCX, 1024], BF16)
    T["kTB"] = K.dout("kTB", [2, 128, TC], BF16)
    T["VB"] = K.dout("VB", [TC, 256], BF16)
    stage1(K, T)
    return K, ["xT0", "qT0", "kTA", "VA", "kTB", "VB"]


class Scope:
    def __init__(self, K):
        self.K = K
        self.st = contextlib.ExitStack()

    def sbuf(self, name, shape, dt):
        return self.K.P.sbuf(name, shape, dt, stack=self.st)

    def close(self):
        self.K.P.barrier()
        self.st.close()


def postnorm_residual(K, sc, y_g, ss, gain, xT_in, xT_out, g, xg, tmp):
    P = K.P
    rstd = tmp["rstd"]
    rstd_from_ss(K, ss, rstd, 1.0 / D)
    P.dma("sp", xg[:], xT_in.ap()[:, :, g * G:(g + 1) * G].rearrange("c p t -> p c t"), reads=[xT_in], writes=[xg])
    for c in range(KC):
        P.op("pool", lambda e, c=c: e.tensor_tensor(out=y_g[:, c, :], in0=y_g[:, c, :], in1=rstd[:], op=ALU.mult),
             reads=[y_g, rstd], writes=[y_g])
        P.op("dve", lambda e, c=c: e.scalar_tensor_tensor(out=xg[:, c, :], in0=y_g[:, c, :], scalar=gain[:, c:c + 1],
                                                           in1=xg[:, c, :], op0=ALU.mult, op1=ALU.add),
             reads=[y_g, gain, xg], writes=[xg])
    if xT_out is not None:
        P.dma("sp", xT_out.ap()[:, :, g * G:(g + 1) * G].rearrange("c p t -> p c t"), xg[:], reads=[xg], pwrites=[xT_out])


def proj_rows_to_y(K, wd, ntile_base, kcn, rhs_fn, rhs_bufs, y_g, wpool, psr, ss, sqr):
    P = K.P
    for c in range(KC):
        wt = load_w(K, wpool, wd, ntile_base + c, kcn, 128)
        ps = psr.next()
        for kc in range(kcn):
            mm(P, ps, ps[:], wt[:, kc, :], rhs_fn(kc), kc == 0, kc == kcn - 1, [wt] + rhs_bufs)
        P.op("act", lambda e, ps=ps, c=c: e.activation(out=y_g[:, c, :], in_=ps[:], func=AF.Copy), reads=[ps], pwrites=[y_g])
        sq = sqr.next()
        P.op("act", lambda e, c=c, sq=sq: e.activation(out=sq[:], in_=y_g[:, c, :], func=AF.Square), reads=[y_g], writes=[sq])
        mm(P, ss, ss[:], K.ones[:], sq[:], c == 0, c == KC - 1, [K.ones, sq])


def mixer_tail(K, T, oT_all, w_out, g_post, g_ffn, xT_in, xT_out, hT_out):
    P = K.P
    sc = Scope(K)
    y_g = sc.sbuf("mt_y", [128, KC, G], F32)
    xg = sc.sbuf("mt_x", [128, KC, G], F32)
    hT_g = sc.sbuf("mt_h", [128, KC, G], BF16)
    sq = sc.sbuf("mt_sq", [128, KC, G], BF16)
    tmp = {"rstd": sc.sbuf("mt_rstd", [128, G], F32)}
    rstd2 = sc.sbuf("mt_rstd2", [128, G], F32)
    sqr = Rot([sc.sbuf("mt_sqc%d" % i, [128, G], BF16) for i in range(2)])
    wpool = Rot([sc.sbuf("mt_w%d" % i, [128, KC, 128], BF16) for i in range(2)])
    K.wstage = Rot([sc.sbuf("mt_ws%d" % i, [128, KC * 128], F32) for i in range(2)])
    psr = Rot(K.ps[0:6])
    for g in range(K.NG):
        ss = K.ps[6]
        proj_rows_to_y(K, w_out, 0, KC, lambda kc, g=g: oT_all[:, kc, g * G:(g + 1) * G], [oT_all], y_g, wpool, psr, ss, sqr)
        postnorm_residual(K, sc, y_g, ss, g_post, xT_in, xT_out, g, xg, tmp)
        norm_group(K, xg, g_ffn, hT_g, 0, Rot([K.ps[7]]), sq, rstd2)
        P.dma("sp", hT_out.ap()[:, :, g * G:(g + 1) * G].rearrange("c p t -> p c t"), hT_g[:], reads=[hT_g], pwrites=[hT_out])
    sc.close()


def stage2(K, T):
    P = K.P
    TC, NG, S, NT = K.TC, K.NG, K.S, K.NT
    scale = 128 ** -0.5
    oT_all = P.sbuf("oT_all", [128, 16, TC], BF16)
    g_post = K.load_const("l0_mix_post", [128, KC])
    g_ffn = K.load_const("l0_ffn_pre", [128, KC])
    sc = Scope(K)
    pbufs = Rot([sc.sbuf("pb%d" % i, [128, G], BF16) for i in range(4)])
    rdbuf = Rot([sc.sbuf("rd%d" % i, [128, G], F32) for i in range(2)])
    ps_s, ps_o, ps_d = Rot(K.ps[0:3]), Rot(K.ps[3:5]), Rot(K.ps[5:7])
    qg = sc.sbuf("qg", [128, 16, G], BF16)
    scA = Scope(K)
    masks = scA.sbuf("masks_sb", [128, 20, G], BF16)
    P.dma("sp", masks[:], T["c_masks"].ap().rearrange("i p q -> p i q"), reads=[T["c_masks"]], writes=[masks])
    denA = scA.sbuf("denA_sb", [128, K.TCX // 128, 128], BF16)
    P.dma("sp", denA[:], T["denA"].ap(), reads=[T["denA"]], writes=[denA])
    kAs = Rot([scA.sbuf("kA%d" % i, [128, 2560], BF16) for i in range(2)])
    vAs = Rot([scA.sbuf("vA%d" % i, [128, 20, 128], BF16) for i in range(2)])
    for g in range(NG):
        P.dma("sp", qg[:], T["qT0"].ap()[:, :, g * G:(g + 1) * G].rearrange("h p t -> p h t"), reads=[T["qT0"]], writes=[qg])
        for h in range(8):
            kA = kAs.next()
            vA = vAs.next()
            P.dma("sp", kA[:], T["kTA"].ap()[h, :, g * G:g * G + 2560], reads=[T["kTA"]], writes=[kA])
            P.dma("sp", vA[:], T["VA"].ap()[g * G:g * G + 2560, h * 128:(h + 1) * 128].rearrange("(t p) c -> p t c", p=128),
                  reads=[T["VA"]], writes=[vA])
            attention_block(K, [qg[:, h, :]], lambda i, kA=kA: [kA[:, i * 128:(i + 1) * 128]],
                            lambda i, vA=vA: (vA[:, i, :], [vA]),
                            lambda i, g=g: (denA[:, g * 4 + i, :], [denA]),
                            20, lambda i: (masks[:, i, :], masks), scale,
                            oT_all[:, h, g * G:(g + 1) * G], oT_all, ps_s, ps_o, ps_d, pbufs, rdbuf, [kA, qg])
    scA.close()
    scB = Scope(K)
    kB = scB.sbuf("kB", [128, 2, S], BF16)
    vB = scB.sbuf("vB", [128, NT, 256], BF16)
    ntr = TC // 128
    for r in range(4):
        P.dma("sp", kB[:, :, r * TC:(r + 1) * TC], T["kTB_all"].ap()[r].rearrange("h p t -> p h t"), reads=[T["kTB_all"]], pwrites=[kB])
        P.dma("sp", vB[:, r * ntr:(r + 1) * ntr, :], T["VB_all"].ap()[r].rearrange("(t p) c -> p t c", p=128), reads=[T["VB_all"]], pwrites=[vB])
    for g in range(NG):
        P.dma("sp", qg[:], T["qT0"].ap()[:, :, g * G:(g + 1) * G].rearrange("h p t -> p h t"), reads=[T["qT0"]], writes=[qg])
        for h in range(8):
            kv = h // 4
            attention_block(K, [qg[:, 8 + h, :]], lambda i, kv=kv: [kB[:, kv, i * 128:(i + 1) * 128]],
                            lambda i, kv=kv: (vB[:, i, kv * 128:(kv + 1) * 128], [vB]),
                            lambda i: (K.ones[:], [K.ones]),
                            NT, None, scale,
                            oT_all[:, 8 + h, g * G:(g + 1) * G], oT_all, ps_s, ps_o, ps_d, pbufs, rdbuf, [kB, qg])
    scB.close()
    sc.close()
    mixer_tail(K, T, oT_all, T["w0_out"], g_post, g_ffn, T["xT0"], T["xT1"], T["hT1"])


def build_stage2(S):
    K = KB(S)
    K.common()
    TC, TCX = K.TC, K.TCX
    T = {}
    T["qT0"] = K.din("qT0", [16, 128, TC], BF16)
    T["kTA"] = K.din("kTA", [8, 128, TCX], BF16)
    T["VA"] = K.din("VA", [TCX, 1024], BF16)
    T["kTB_all"] = K.din("kTB_all", [4, 2, 128, TC], BF16)
    T["VB_all"] = K.din("VB_all", [4, TC, 256], BF16)
    T["c_masks"] = K.din("c_masks", [20, 128, G], BF16)
    T["denA"] = K.din("denA", [128, TCX // 128, 128], BF16)
    T["w0_out"] = K.din("w0_out", [16, 128, KC, 128])
    T["xT0"] = K.din("xT0", [KC, 128, TC], F32)
    T["xT1"] = K.dout("xT1", [KC, 128, TC], F32)
    T["hT1"] = K.dout("hT1", [KC, 128, TC], BF16)
    stage2(K, T)
    return K, ["xT1", "hT1"]


def prep_out_weights(inp, name, key):
    w = inp[name]
    return {key: np.stack([fm_tile(w, c * 128, 128) for c in range(16)])}


def ffn_pass(K, T, L, xT_in, xT_out, y_out):
    P = K.P
    NG, TC = K.NG, K.TC
    cw = K.load_const("l%d_cw" % L, [128, NFF, 3])
    cb = K.load_const("l%d_cb" % L, [128, NFF])
    g_post = K.load_const("l%d_ffn_post" % L, [128, KC])
    sc = Scope(K)
    hTx = sc.sbuf("f_hTx", [128, KC, G + 2], BF16)
    hs = sc.sbuf("f_hs", [128, KC, 2], BF16)
    aT = sc.sbuf("f_aT", [128, NFF, G], BF16)
    y_g = sc.sbuf("f_y", [128, KC, G], F32)
    xg = sc.sbuf("f_x", [128, KC, G], F32)
    tmp = {"rstd": sc.sbuf("f_rstd", [128, G], F32)}
    wup = Rot([sc.sbuf("f_wup%d" % i, [128, KC, 256], BF16) for i in range(2)])
    wdn = Rot([sc.sbuf("f_wdn%d" % i, [128, NFF, 128], BF16) for i in range(1)])
    K.wstage = Rot([sc.sbuf("f_ws%d" % i, [128, NFF * 128], F32) for i in range(1)])
    gsbs = Rot([sc.sbuf("f_gsb%d" % i, [128, G + 2], F32) for i in range(2)])
    accs = Rot([sc.sbuf("f_acc%d" % i, [128, G], F32) for i in range(2)])
    gls = Rot([sc.sbuf("f_gl%d" % i, [128, G], F32) for i in range(2)])
    sqr = Rot([sc.sbuf("f_sqc%d" % i, [128, G], BF16) for i in range(2)])
    hT, halo, w_up, w_dn = T["hT_ffn"], T["h_halo"], T["w_up"], T["w_dn"]
    P.dma("sp", hs[:], halo.ap(), reads=[halo], writes=[hs])
    psG, psV, psH = Rot(K.ps[0:2]), Rot(K.ps[2:4]), Rot([K.ps[4]])
    psr = Rot(K.ps[4:6])
    if y_out is not None:
        yblks = Rot([sc.sbuf("f_yb%d" % i, [128, D], F32) for i in range(1)])
    for g in range(NG):
        lo = g * G - 1
        hi = g * G + G + 1
        slo, shi = max(lo, 0), min(hi, TC)
        P.dma("sp", hTx[:, :, slo - lo:shi - lo], hT.ap()[:, :, slo:shi].rearrange("c p t -> p c t"), reads=[hT], pwrites=[hTx])
        if g == 0:
            P.op("dve", lambda e: e.tensor_copy(hTx[:, :, 0:1], hs[:, :, 0:1]), reads=[hs], pwrites=[hTx])
        if g == NG - 1:
            P.op("dve", lambda e: e.tensor_copy(hTx[:, :, G + 1:G + 2], hs[:, :, 1:2]), reads=[hs], pwrites=[hTx])
        for j in range(NFF):
            wt = load_w(K, wup, w_up, j, KC, 256)
            pg, pv, ph = psG.next(), psV.next(), psH.next()
            for kc in range(KC):
                mm(P, pg, pg[:], wt[:, kc, 0:128], hTx[:, kc, 1:G + 1], kc == 0, kc == KC - 1, [wt, hTx])
            for kc in range(KC):
                mm(P, ph, ph[:, 0:1], wt[:, kc, 0:128], hTx[:, kc, 0:1], kc == 0, kc == KC - 1, [wt, hTx])
            for kc in range(KC):
                mm(P, ph, ph[:, 2:3], wt[:, kc, 0:128], hTx[:, kc, G + 1:G + 2], kc == 0, kc == KC - 1, [wt, hTx])
            for kc in range(KC):
                mm(P, pv, pv[:], wt[:, kc, 128:256], hTx[:, kc, 1:G + 1], kc == 0, kc == KC - 1, [wt, hTx])
            gsb, acc, gl = gsbs.next(), accs.next(), gls.next()
            P.op("act", lambda e, gsb=gsb, pg=pg: e.activation(out=gsb[:, 1:G + 1], in_=pg[:], func=AF.Copy), reads=[pg], pwrites=[gsb])
            P.op("dve", lambda e, gsb=gsb, ph=ph: e.tensor_copy(gsb[:, 0:1], ph[:, 0:1]), reads=[ph], pwrites=[gsb])
            P.op("dve", lambda e, gsb=gsb, ph=ph: e.tensor_copy(gsb[:, G + 1:G + 2], ph[:, 2:3]), reads=[ph], pwrites=[gsb])
            P.op("act", lambda e, gsb=gsb, acc=acc, j=j: e.activation(out=acc[:], in_=gsb[:, 1:G + 1], func=AF.Identity,
                                                                      bias=cb[:, j:j + 1], scale=cw[:, j, 1:2]),
                 reads=[gsb, cb, cw], writes=[acc])
            P.op("dve", lambda e, gsb=gsb, acc=acc, j=j: e.scalar_tensor_tensor(out=acc[:], in0=gsb[:, 0:G], scalar=cw[:, j, 0:1], in1=acc[:],
                                                                               op0=ALU.mult, op1=ALU.add), reads=[gsb, cw, acc], writes=[acc])
            P.op("dve", lambda e, gsb=gsb, acc=acc, j=j: e.scalar_tensor_tensor(out=acc[:], in0=gsb[:, 2:G + 2], scalar=cw[:, j, 2:3], in1=acc[:],
                                                                               op0=ALU.mult, op1=ALU.add), reads=[gsb, cw, acc], writes=[acc])
            P.op("act", lambda e, acc=acc, gl=gl: e.activation(out=gl[:], in_=acc[:], func=AF.Gelu), reads=[acc], writes=[gl])
            P.op("dve", lambda e, gl=gl, pv=pv, j=j: e.tensor_tensor(out=aT[:, j, :], in0=gl[:], in1=pv[:], op=ALU.mult),
                 reads=[gl, pv], pwrites=[aT])
        ss = K.ps[6]
        proj_rows_to_y(K, w_dn, 0, NFF, lambda kc: aT[:, kc, :], [aT], y_g, wdn, psr, ss, sqr)
        postnorm_residual(K, sc, y_g, ss, g_post, xT_in, xT_out, g, xg, tmp)
        if y_out is not None:
            pst = Rot([K.ps[7], K.ps[5]])
            for tb in range(4):
                yb = yblks.next()
                for cq in range(4):
                    ps = pst.next()
                    for c4 in range(4):
                        c = cq * 4 + c4
                        P.op("pe", lambda e, ps=ps, c=c, c4=c4, tb=tb: e.transpose(ps[:, c4 * 128:(c4 + 1) * 128],
                                                                                    xg[:, c, tb * 128:(tb + 1) * 128], K.identf[:]),
                             reads=[xg, K.identf], pwrites=[ps])
                    P.op("dve", lambda e, ps=ps, yb=yb, cq=cq: e.tensor_copy(yb[:, cq * 512:(cq + 1) * 512], ps[:]), reads=[ps], pwrites=[yb])
                P.dma("sp", y_out.ap()[g * G + tb * 128:g * G + (tb + 1) * 128, :], yb[:], reads=[yb], pwrites=[y_out])
    sc.close()


def l1_pre(K, T):
    P = K.P
    NG, TC = K.NG, K.TC
    gpre = K.load_const("l1_mix_pre", [128, KC])
    gqa = K.load_const("l1_q_a_norm", [128, 4])
    gkva = K.load_const("l1_kv_a_norm", [128, 4])
    permC = K.load_const("c_permC", [128, 128], BF16)
    sc = Scope(K)
    xg = sc.sbuf("p_x", [128, KC, G], F32)
    hT_g = sc.sbuf("p_h", [128, KC, G], BF16)
    sq = sc.sbuf("p_sq", [128, KC, G], BF16)
    rstd = sc.sbuf("p_rstd", [128, G], F32)
    cq = sc.sbuf("p_cq", [128, 8, G], F32)
    cn = sc.sbuf("p_cn", [128, 8, G], BF16)
    wfm = Rot([sc.sbuf("p_w%d" % i, [128, KC, 128], BF16) for i in range(1)])
    wq = Rot([sc.sbuf("p_wq%d" % i, [128, 4, 192], BF16) for i in range(2)])
    wv = Rot([sc.sbuf("p_wv%d" % i, [128, 4, 512], BF16) for i in range(2)])
    K.wstage = Rot([sc.sbuf("p_ws%d" % i, [128, KC * 128], F32) for i in range(1)])
    tb_f = Rot([sc.sbuf("p_tf%d" % i, [128, G], F32) for i in range(3)])
    tb_bf = Rot([sc.sbuf("p_tb%d" % i, [128, G], BF16) for i in range(3)])
    sqr = Rot([sc.sbuf("p_sqc%d" % i, [128, G], BF16) for i in range(2)])
    rq = sc.sbuf("p_rq", [128, G], F32)
    rkv = sc.sbuf("p_rkv", [128, G], F32)
    qst = sc.sbuf("p_qst", [128, 16, G], BF16)
    qrst = sc.sbuf("p_qrst", [64, 16, G], BF16)
    kst = sc.sbuf("p_kst", [128, 16, G], BF16)
    krst = sc.sbuf("p_krst", [64, G], BF16)
    vst = sc.sbuf("p_vst", [128, 4, 2048], BF16)
    cs = Rot([sc.sbuf("p_cs%d" % i, [64, 2, G], F32) for i in range(2)])
    psr = Rot(K.ps[0:6])
    xT2 = T["xT2"]
    for g in range(NG):
        P.dma("sp", xg[:], xT2.ap()[:, :, g * G:(g + 1) * G].rearrange("c p t -> p c t"), reads=[xT2], writes=[xg])
        norm_group(K, xg, gpre, hT_g, 0, Rot([K.ps[7]]), sq, rstd)
        c_ = cs.next()
        P.dma("sp", c_[:, 0, :], T["cosC"].ap()[:, g * G:(g + 1) * G], reads=[T["cosC"]], pwrites=[c_])
        P.dma("sp", c_[:, 1, :], T["sinC"].ap()[:, g * G:(g + 1) * G], reads=[T["sinC"]], pwrites=[c_])
        ssq, sskv = K.ps[6], K.ps[7]
        for t in range(9):
            wt = load_w(K, wfm, T["w1_in"], t, KC, 128)
            ps = psr.next()
            for kc in range(KC):
                mm(P, ps, ps[:], wt[:, kc, :], hT_g[:, kc, :], kc == 0, kc == KC - 1, [wt, hT_g])
            if t < 8:
                P.op("act", lambda e, ps=ps, t=t: e.activation(out=cq[:, t, :], in_=ps[:], func=AF.Copy), reads=[ps], pwrites=[cq])
                s_ = sqr.next()
                P.op("act", lambda e, t=t, s_=s_: e.activation(out=s_[:], in_=cq[:, t, :], func=AF.Square), reads=[cq], writes=[s_])
                acc = ssq if t < 4 else sskv
                mm(P, acc, acc[:], K.ones[:], s_[:], t % 4 == 0, t % 4 == 3, [K.ones, s_])
            else:
                rope_apply(K, ps[:64, :], ps, 64, c_[:, 0, :], c_[:, 1, :], [c_], permC, krst[:, :], krst, psr, tb_bf, tb_f)
                P.dma("sp", T["krT"].ap()[:, g * G:(g + 1) * G], krst[:], reads=[krst], pwrites=[T["krT"]])
        rstd_from_ss(K, ssq, rq, 1.0 / 512)
        rstd_from_ss(K, sskv, rkv, 1.0 / 512)
        for t in range(8):
            gg, rr = (gqa, rq) if t < 4 else (gkva, rkv)
            P.op("dve", lambda e, t=t, gg=gg, rr=rr: e.scalar_tensor_tensor(out=cn[:, t, :], in0=cq[:, t, :], scalar=gg[:, t % 4:t % 4 + 1],
                                                                            in1=rr[:], op0=ALU.mult, op1=ALU.mult),
                 reads=[cq, gg, rr], pwrites=[cn])
        for h in range(16):
            wt = load_w(K, wq, T["w1_uq"], h, 4, 192)
            ps = psr.next()
            for kc in range(4):
                mm(P, ps, ps[:], wt[:, kc, 0:128], cn[:, kc, :], kc == 0, kc == 3, [wt, cn])
            P.op("act", lambda e, ps=ps, h=h: e.activation(out=qst[:, h, :], in_=ps[:], func=AF.Copy), reads=[ps], pwrites=[qst])
            ps = psr.next()
            for kc in range(4):
                mm(P, ps, ps[:64, :], wt[:, kc, 128:192], cn[:, kc, :], kc == 0, kc == 3, [wt, cn])
            rope_apply(K, ps[:64, :], ps, 64, c_[:, 0, :], c_[:, 1, :], [c_], permC, qrst[:, h, :], qrst, psr, tb_bf, tb_f)
        P.dma("sp", T["qT1"].ap()[:, :, g * G:(g + 1) * G].rearrange("h p t -> p h t"), qst[:], reads=[qst], pwrites=[T["qT1"]])
        P.dma("sp", T["qrT1"].ap()[:, :, g * G:(g + 1) * G].rearrange("h p t -> p h t"), qrst[:], reads=[qrst], pwrites=[T["qrT1"]])
        for h in range(16):
            wt = load_w(K, wfm, T["w1_uk"], h, 4, 128)
            ps = psr.next()
            for kc in range(4):
                mm(P, ps, ps[:], wt[:, kc, :], cn[:, 4 + kc, :], kc == 0, kc == 3, [wt, cn])
            P.op("act", lambda e, ps=ps, h=h: e.activation(out=kst[:, h, :], in_=ps[:], func=AF.Copy), reads=[ps], pwrites=[kst])
        P.dma("sp", T["kTC"].ap()[:, :, g * G:(g + 1) * G].rearrange("h p t -> p h t"), kst[:], reads=[kst], pwrites=[T["kTC"]])
        for vt in range(4):
            wt = load_w(K, wv, T["w1_uv"], vt, 4, 512)
            for tb in range(4):
                ps = psr.next()
                for kc in range(4):
                    mm(P, ps, ps[:], cn[:, 4 + kc, tb * 128:(tb + 1) * 128], wt[:, kc, :], kc == 0, kc == 3, [wt, cn])
                P.op("act", lambda e, ps=ps, tb=tb, vt=vt: e.activation(out=vst[:, tb, vt * 512:(vt + 1) * 512], in_=ps[:], func=AF.Copy),
                     reads=[ps], pwrites=[vst])
        P.dma("sp", T["VC"].ap()[g * G:(g + 1) * G, :].rearrange("(tb p) c -> p tb c", p=128), vst[:], reads=[vst], pwrites=[T["VC"]])
    sc.close()


def stage4(K, T):
    P = K.P
    TC, NG, S, NT = K.TC, K.NG, K.S, K.NT
    scale = 192 ** -0.5
    oT_all = P.sbuf("oT_all", [128, 16, TC], BF16)
    g_post = K.load_const("l1_mix_post", [128, KC])
    g_ffn = K.load_const("l1_ffn_pre", [128, KC])
    sc = Scope(K)
    pbufs = Rot([sc.sbuf("pb%d" % i, [128, G], BF16) for i in range(4)])
    rdbuf = Rot([sc.sbuf("rd%d" % i, [128, G], F32) for i in range(2)])
    ps_s, ps_o, ps_d = Rot(K.ps[0:3]), Rot(K.ps[3:5]), Rot(K.ps[5:7])
    kr = sc.sbuf("c_kr", [64, S], BF16)
    khs = Rot([sc.sbuf("c_kh%d" % i, [128, S], BF16) for i in range(2)])
    vhs = Rot([sc.sbuf("c_vh%d" % i, [128, NT, 128], BF16) for i in range(2)])
    qhs = Rot([sc.sbuf("c_qh%d" % i, [128, TC], BF16) for i in range(2)])
    qrs = Rot([sc.sbuf("c_qr%d" % i, [64, TC], BF16) for i in range(2)])
    ntr = TC // 128
    for r in range(4):
        P.dma("sp", kr[:, r * TC:(r + 1) * TC], T["krT_all"].ap()[r], reads=[T["krT_all"]], pwrites=[kr])
    for h in range(16):
        kh, vh, qh, qr = khs.next(), vhs.next(), qhs.next(), qrs.next()
        for r in range(4):
            P.dma("sp", kh[:, r * TC:(r + 1) * TC], T["kTC_all"].ap()[r, h], reads=[T["kTC_all"]], pwrites=[kh])
            P.dma("sp", vh[:, r * ntr:(r + 1) * ntr, :], T["VC_all"].ap()[r, :, h * 128:(h + 1) * 128].rearrange("(t p) c -> p t c", p=128),
                  reads=[T["VC_all"]], pwrites=[vh])
        P.dma("sp", qh[:], T["qT1"].ap()[h], reads=[T["qT1"]], writes=[qh])
        P.dma("sp", qr[:], T["qrT1"].ap()[h], reads=[T["qrT1"]], writes=[qr])
        for g in range(NG):
            attention_block(K, [qh[:, g * G:(g + 1) * G], qr[:, g * G:(g + 1) * G]],
                            lambda i, kh=kh: [kh[:, i * 128:(i + 1) * 128], kr[:, i * 128:(i + 1) * 128]],
                            lambda i, vh=vh: (vh[:, i, :], [vh]),
                            lambda i: (K.ones[:], [K.ones]),
                            NT, None, scale,
                            oT_all[:, h, g * G:(g + 1) * G], oT_all, ps_s, ps_o, ps_d, pbufs, rdbuf, [kh, kr, qh, qr])
    sc.close()
    mixer_tail(K, T, oT_all, T["w1_out"], g_post, g_ffn, T["xT2"], T["xT3"], T["hT3"])


def build_stage3(S):
    K = KB(S)
    K.common()
    TC = K.TC
    T = {}
    T["hT_ffn"] = K.din("hT_ffn", [KC, 128, TC], BF16)
    T["h_halo"] = K.din("h_halo", [128, KC, 2], BF16)
    T["w_up"] = K.din("w_up", [NFF, 128, KC, 256])
    T["w_dn"] = K.din("w_dn", [16, 128, NFF, 128])
    xT1 = K.din("xT_in", [KC, 128, TC], F32)
    T["xT2"] = K.dout("xT2", [KC, 128, TC], F32)
    ffn_pass(K, T, 0, xT1, T["xT2"], None)
    T["w1_in"] = K.din("w1_in", [9, 128, KC, 128])
    T["w1_uq"] = K.din("w1_uq", [16, 128, 4, 192])
    T["w1_uk"] = K.din("w1_uk", [16, 128, 4, 128])
    T["w1_uv"] = K.din("w1_uv", [4, 128, 4, 512])
    T["cosC"] = K.din("cosC", [64, TC])
    T["sinC"] = K.din("sinC", [64, TC])
    T["qT1"] = K.dout("qT1", [16, 128, TC], BF16)
    T["qrT1"] = K.dout("qrT1", [16, 64, TC], BF16)
    T["kTC"] = K.dout("kTC", [16, 128, TC], BF16)
    T["krT"] = K.dout("krT", [64, TC], BF16)
    T["VC"] = K.dout("VC", [TC, 2048], BF16)
    l1_pre(K, T)
    return K, ["xT2", "qT1", "qrT1", "kTC", "krT", "VC"]


def build_stage4(S):
    K = KB(S)
    K.common()
    TC = K.TC
    T = {}
    T["qT1"] = K.din("qT1", [16, 128, TC], BF16)
    T["qrT1"] = K.din("qrT1", [16, 64, TC], BF16)
    T["kTC_all"] = K.din("kTC_all", [4, 16, 128, TC], BF16)
    T["krT_all"] = K.din("krT_all", [4, 64, TC], BF16)
    T["VC_all"] = K.din("VC_all", [4, TC, 2048], BF16)
    T["w1_out"] = K.din("w1_out", [16, 128, KC, 128])
    T["xT2"] = K.din("xT2", [KC, 128, TC], F32)
    T["xT3"] = K.dout("xT3", [KC, 128, TC], F32)
    T["hT3"] = K.dout("hT3", [KC, 128, TC], BF16)
    stage4(K, T)
    return K, ["xT3", "hT3"]


def build_stage5(S):
    K = KB(S)
    K.common()
    TC = K.TC
    T = {}
    T["hT_ffn"] = K.din("hT_ffn", [KC, 128, TC], BF16)
    T["h_halo"] = K.din("h_halo", [128, KC, 2], BF16)
    T["w_up"] = K.din("w_up", [NFF, 128, KC, 256])
    T["w_dn"] = K.din("w_dn", [16, 128, NFF, 128])
    xT3 = K.din("xT_in", [KC, 128, TC], F32)
    xT4 = K.dout("xT4", [KC, 128, TC], F32)
    ffn_pass(K, T, 1, xT3, xT4, None)
    return K, ["xT4"]


def prep_ffn_weights(inp, L):
    p = "l%d_" % L
    wu, wd = inp[p + "w_up"], inp[p + "w_down"]
    o = {}
    o["w_up"] = np.stack([np.concatenate([fm_tile(wu, j * 128, 128), fm_tile(wu, DFF + j * 128, 128)], axis=2) for j in range(NFF)])
    o["w_dn"] = np.stack([fm_tile(wd, c * 128, 128) for c in range(16)])
    cw = inp[p + "conv_w"]
    o[p + "cw"] = np.ascontiguousarray(cw.reshape(3, NFF, 128).transpose(2, 1, 0))
    o[p + "cb"] = vec_tile(inp[p + "conv_b"])
    o[p + "ffn_post"] = vec_tile(inp[p + "ffn_post"])
    return o


def prep_l1_weights(inp):
    o = {}
    w = inp["l1_w_in"]
    wpad = np.zeros((D, 9 * 128), dtype=np.float32)
    wpad[:, :1088] = w
    o["w1_in"] = np.stack([fm_tile(wpad, t * 128, 128) for t in range(9)])
    o["w1_uq"] = np.stack([fm_tile(inp["l1_w_uq"], h * 192, 192) for h in range(16)])
    wkv = inp["l1_w_ukv"]
    o["w1_uk"] = np.stack([fm_tile(wkv, h * 256, 128) for h in range(16)])
    vcols = np.concatenate([np.arange(h * 256 + 128, h * 256 + 256) for h in range(16)])
    wv = np.ascontiguousarray(wkv[:, vcols])
    o["w1_uv"] = np.stack([fm_tile(wv, vt * 512, 512) for vt in range(4)])
    o["l1_mix_pre"] = vec_tile(inp["l1_mix_pre"])
    o["l1_q_a_norm"] = vec_tile(inp["l1_q_a_norm"])
    o["l1_kv_a_norm"] = vec_tile(inp["l1_kv_a_norm"])
    return o


def halo_cols(hT_list, S):
    out = []
    for c in range(8):
        h = np.zeros((128, KC, 2), dtype=NPBF)
        if c % 4 != 0:
            h[:, :, 0] = hT_list[c - 1][:, :, -1].T
        if c % 4 != 3:
            h[:, :, 1] = hT_list[c + 1][:, :, 0].T
        out.append(h)
    return out


def kernel(**inputs):
    inp = {k: np.asarray(v) for k, v in inputs.items()}
    x = inp["x"]
    B, S, _ = x.shape
    TC = S // 4
    cst = host_consts(S)
    cc = [core_consts(S, c) for c in range(8)]
    K, outs = build_stage1(S)
    w0 = prep_l0_weights(inp)
    pc = []
    for c in range(8):
        d = dict(cst); d.update(w0); d.update(cc[c]); d["x_ext"] = x_ext_for(x, S, c)
        pc.append(d)
    r1 = run_launch(K, pc, outs)
    del w0
    K, outs = build_stage2(S)
    wo = prep_out_weights(inp, "l0_w_out", "w0_out")
    wo["l0_mix_post"] = vec_tile(inp["l0_mix_post"]); wo["l0_ffn_pre"] = vec_tile(inp["l0_ffn_pre"])
    pc = []
    for c in range(8):
        b = c // 4
        d = dict(cst); d.update(wo); d.update(cc[c])
        for n in ("qT0", "kTA", "VA", "xT0"):
            d[n] = r1[c][n]
        d["kTB_all"] = np.stack([r1[b * 4 + r]["kTB"] for r in range(4)])
        d["VB_all"] = np.stack([r1[b * 4 + r]["VB"] for r in range(4)])
        pc.append(d)
    r2 = run_launch(K, pc, outs)
    del r1
    K, outs = build_stage3(S)
    wf = prep_ffn_weights(inp, 0)
    wf.update(prep_l1_weights(inp))
    hal = halo_cols([r["hT1"] for r in r2], S)
    pc = []
    for c in range(8):
        d = dict(cst); d.update(wf); d.update(cc[c])
        d["hT_ffn"] = r2[c]["hT1"]; d["h_halo"] = hal[c]; d["xT_in"] = r2[c]["xT1"]
        pc.append(d)
    r3 = run_launch(K, pc, outs)
    del r2, wf
    K, outs = build_stage4(S)
    wo = prep_out_weights(inp, "l1_w_out", "w1_out")
    wo["l1_mix_post"] = vec_tile(inp["l1_mix_post"]); wo["l1_ffn_pre"] = vec_tile(inp["l1_ffn_pre"])
    pc = []
    for c in range(8):
        b = c // 4
        d = dict(cst); d.update(wo); d.update(cc[c])
        for n in ("qT1", "qrT1", "xT2"):
            d[n] = r3[c][n]
        d["kTC_all"] = np.stack([r3[b * 4 + r]["kTC"] for r in range(4)])
        d["krT_all"] = np.stack([r3[b * 4 + r]["krT"] for r in range(4)])
        d["VC_all"] = np.stack([r3[b * 4 + r]["VC"] for r in range(4)])
        pc.append(d)
    r4 = run_launch(K, pc, outs)
    del r3
    K, outs = build_stage5(S)
    wf = prep_ffn_weights(inp, 1)
    hal = halo_cols([r["hT3"] for r in r4], S)
    pc = []
    for c in range(8):
        d = dict(cst); d.update(wf); d.update(cc[c])
        d["hT_ffn"] = r4[c]["hT3"]; d["h_halo"] = hal[c]; d["xT_in"] = r4[c]["xT3"]
        pc.append(d)
    r5 = run_launch(K, pc, outs)
    out = np.zeros((B, S, D), dtype=np.float32)
    for c in range(8):
        out[c // 4, (c % 4) * TC:(c % 4 + 1) * TC] = r5[c]["xT4"].transpose(2, 0, 1).reshape(TC, D)
    return out
```

```python
import contextlib
import os
import numpy as np
import ml_dtypes
import concourse.bass as bass
import concourse.mybir as mybir
from concourse.bass_utils import run_bass_kernel_spmd

F32 = mybir.dt.float32
BF16 = mybir.dt.bfloat16
ALU = mybir.AluOpType
AF = mybir.ActivationFunctionType
AX = mybir.AxisListType
NPBF = ml_dtypes.bfloat16

D = 2048
KC = 16
DFF = 5632
NFF = 44
EPS = 1e-6
HALO = 1024
G = 512


class Buf:
    def __init__(self, t, name):
        self.t = t
        self.name = name
        self.writers = []
        self.readers = []
        self.gen_deps = set()
        self.sem = None
        self.sem_cnt = 0
        self.sem_hist = []
        self.is_psum = False

    def __getitem__(self, idx):
        return self.t[idx]

    def ap(self):
        return self.t.ap() if hasattr(self.t, "ap") else self.t[:]


class Ins:
    __slots__ = ("eng", "fn", "deps", "is_dma", "dsem", "dcnt", "sig", "idx", "bar")


class Prog:
    ENGS = ("pe", "act", "dve", "pool", "sp")

    def __init__(self, nc, stack):
        self.nc = nc
        self.stack = stack
        self.ins = []
        self.n_dma_sems = 0
        self.all_bufs = []
        self.barriers = []
        self.cur_bar = -1

    def sbuf(self, name, shape, dt, stack=None):
        st = stack or self.stack
        b = Buf(st.enter_context(self.nc.sbuf_tensor(name, list(shape), dt)), name)
        self.all_bufs.append(b)
        return b

    def psum(self, name, shape, dt=F32):
        b = Buf(self.stack.enter_context(self.nc.psum_tensor(name, list(shape), dt)), name)
        b.is_psum = True
        self.all_bufs.append(b)
        return b

    def dram(self, name, shape, dt, kind="Internal"):
        b = Buf(self.nc.dram_tensor(name, list(shape), dt, kind=kind), name)
        self.all_bufs.append(b)
        return b

    def _compress(self, lst):
        last = {}
        out = []
        for i in lst:
            it = self.ins[i]
            if it.is_dma:
                out.append(i)
            else:
                last[it.eng] = i
        out.extend(last.values())
        return out

    def barrier(self):
        last = {}
        for it in self.ins:
            if not it.is_dma:
                last[it.eng] = it.idx
        snap = []
        for b in self.all_bufs:
            for sm, c in b.sem_hist:
                if c > 0:
                    snap.append((sm, c))
        self.barriers.append((set(last.values()), snap))
        self.cur_bar = len(self.barriers) - 1

    def op(self, eng, fn, reads=(), writes=(), pwrites=(), dma=False):
        i = len(self.ins)
        it = Ins()
        it.eng = eng; it.fn = fn; it.is_dma = dma; it.idx = i
        it.dsem = None; it.dcnt = 0; it.sig = None; it.bar = self.cur_bar
        deps = set()
        xr = [b for b in reads if b.is_psum]
        if xr:
            reads = [b for b in reads if not b.is_psum]
            writes = list(writes) + xr
        for b in reads:
            deps.update(b.writers)
        for b in writes:
            b.gen_deps = set(b.readers) | set(b.writers)
            deps.update(b.gen_deps)
        for b in pwrites:
            if b.readers or not b.writers:
                b.gen_deps = set(b.readers) | set(b.writers)
                b.writers = []
                b.readers = []
            deps.update(b.gen_deps)
        it.deps = deps
        self.ins.append(it)
        for b in reads:
            b.readers.append(i)
            if len(b.readers) > 4:
                b.readers = self._compress(b.readers)
        for b in writes:
            b.writers = [i]
            b.readers = []
        for b in pwrites:
            b.writers.append(i)
            if len(b.writers) > 4:
                b.writers = self._compress(b.writers)
        if dma:
            tgt = (list(writes) + list(pwrites))[0]
            if tgt.sem is None or tgt.sem_cnt >= 16000:
                tgt.sem = self.stack.enter_context(self.nc.semaphore("ds%d" % self.n_dma_sems))
                tgt.sem_cnt = 0
                tgt.sem_hist.append([tgt.sem, 0])
                self.n_dma_sems += 1
            tgt.sem_cnt += 16
            tgt.sem_hist[-1][1] = tgt.sem_cnt
            it.dsem = tgt.sem
            it.dcnt = tgt.sem_cnt
        return i

    def dma(self, q, out_ap, in_ap, reads, writes=(), pwrites=(), **kw):
        return self.op(q, lambda e: e.dma_start(out=out_ap, in_=in_ap, **kw),
                       reads=reads, writes=writes, pwrites=pwrites, dma=True)

    def emit(self, final_wait=()):
        nc = self.nc
        ins = self.ins
        need = [False] * len(ins)
        for it in ins:
            for d in it.deps:
                dd = ins[d]
                if dd.is_dma:
                    continue
                if dd.eng == "pe" and it.eng == "pe" and not it.is_dma:
                    continue
                need[d] = True
        for cdeps, _ in self.barriers:
            for d in cdeps:
                need[d] = True
        cnt = {e: 0 for e in self.ENGS}
        esems = {e: [] for e in self.ENGS}
        for it in ins:
            if not it.is_dma and need[it.idx]:
                cnt[it.eng] += 1
                k = (cnt[it.eng] - 1) // 16000
                if k >= len(esems[it.eng]):
                    esems[it.eng].append(self.stack.enter_context(nc.semaphore("es_%s%d" % (it.eng, k))))
                it.sig = (esems[it.eng][k], cnt[it.eng] - k * 16000, k)
        self.sig_counts = dict(cnt)
        streams = {e: [] for e in self.ENGS}
        for it in ins:
            streams[it.eng].append(it)
        self.n_waits = 0
        final_targets = []
        for b in final_wait:
            for sm, c in b.sem_hist:
                final_targets.append((sm, c))

        def run_stream(ename, eng):
            waited = {}
            bar_done = -1

            def do_wait(key, s, v):
                if waited.get(key, 0) >= v:
                    return
                eng.wait_ge(s, v)
                waited[key] = v
                self.n_waits += 1

            for it in streams[ename]:
                while bar_done < it.bar:
                    bar_done += 1
                    cdeps, snap = self.barriers[bar_done]
                    for d in cdeps:
                        dd = ins[d]
                        do_wait(("e", dd.eng, dd.sig[2]), dd.sig[0], dd.sig[1])
                    for sm, c in snap:
                        do_wait(("d", id(sm)), sm, c)
                reqs = {}
                for d in it.deps:
                    dd = ins[d]
                    if dd.is_dma:
                        key = ("d", id(dd.dsem))
                        s, v = dd.dsem, dd.dcnt
                    else:
                        if dd.eng == "pe" and ename == "pe" and not it.is_dma:
                            continue
                        key = ("e", dd.eng, dd.sig[2])
                        s, v = dd.sig[0], dd.sig[1]
                    if key not in reqs or reqs[key][1] < v:
                        reqs[key] = (s, v)
                for key, (s, v) in reqs.items():
                    do_wait(key, s, v)
                bi = it.fn(eng)
                if it.is_dma:
                    bi.then_inc(it.dsem, 16)
                elif it.sig:
                    bi.then_inc(it.sig[0], 1)
            if ename == "sp":
                for s, v in final_targets:
                    eng.wait_ge(s, v)

        with nc.Block() as block:
            @block.tensor
            def _(e):
                run_stream("pe", e)

            @block.scalar
            def _(e):
                run_stream("act", e)

            @block.vector
            def _(e):
                run_stream("dve", e)

            @block.gpsimd
            def _(e):
                run_stream("pool", e)

            @block.sync
            def _(e):
                run_stream("sp", e)


class Rot:
    def __init__(self, bufs):
        self.bufs = bufs
        self.i = 0

    def next(self):
        b = self.bufs[self.i % len(self.bufs)]
        self.i += 1
        return b


class KB:
    def __init__(self, S):
        self.S = S
        self.TC = S // 4
        self.NG = self.TC // G
        self.TCX = self.TC + 2 * HALO
        self.NT = S // 128
        self.nc = bass.Bass("TRN2", target_bir_lowering=False)
        self.stack = contextlib.ExitStack()
        self.P = Prog(self.nc, self.stack)
        self.ext_in = {}
        self.ext_out = {}

    def din(self, name, shape, dt=F32):
        b = self.P.dram(name, shape, dt, kind="ExternalInput")
        self.ext_in[name] = b
        return b

    def dout(self, name, shape, dt=F32):
        b = self.P.dram(name, shape, dt, kind="ExternalOutput")
        self.ext_out[name] = b
        return b

    def common(self):
        P = self.P
        self.ps = [P.psum("ps%d" % i, [128, 512], F32) for i in range(8)]
        self.identf = P.sbuf("identf", [128, 128], F32)
        self.ones = P.sbuf("ones", [128, 128], BF16)
        self.epsb = P.sbuf("epsb", [128, 1], F32)
        idd = self.din("c_ident", [128, 128], F32)
        P.dma("sp", self.identf[:], idd.ap(), reads=[idd], writes=[self.identf])
        P.op("dve", lambda e: e.memset(self.ones[:], 1.0), writes=[self.ones])
        P.op("dve", lambda e: e.memset(self.epsb[:], EPS), writes=[self.epsb])

    def load_const(self, name, shape, dt=F32, q="sp"):
        d = self.din(name, shape, dt)
        sb = self.P.sbuf("sb_" + name, shape, dt)
        self.P.dma(q, sb[:], d.ap(), reads=[d], writes=[sb])
        return sb


def mm(P, ps, out_ap, lhsT, rhs, start, stop, reads):
    P.op("pe", lambda e: e.matmul(out_ap, lhsT=lhsT, rhs=rhs, start=start, stop=stop), reads=reads, pwrites=[ps])


def rstd_from_ss(K, ss_ps, rstd, scale):
    P = K.P
    P.op("act", lambda e: e.activation(out=rstd[:], in_=ss_ps[:], func=AF.Sqrt, bias=K.epsb[:], scale=scale),
         reads=[ss_ps, K.epsb], writes=[rstd])
    P.op("dve", lambda e: e.reciprocal(rstd[:], rstd[:]), reads=[rstd], writes=[rstd])


def norm_group(K, xT_g, gain, hT, hcol0, psr, sq, rstd):
    P = K.P
    import os
    NF = int(os.environ.get("S1_NORM", "15"))
    if NF & 1:
        P.op("act", lambda e: e.activation(out=sq[:], in_=xT_g[:], func=AF.Square), reads=[xT_g], writes=[sq])
    ss = psr.next()
    if os.environ.get("S1_SKIPB"):
        ss = psr.next()
    if NF & 2:
      for c in range(KC):
        mm(P, ss, ss[:], K.ones[:], sq[:, c, :], c == 0, c == KC - 1, [K.ones, sq])
    if NF & 4:
        rstd_from_ss(K, ss, rstd, 1.0 / D)
    if not (NF & 8):
        return
    for c in range(KC):
        P.op("dve", lambda e, c=c: e.scalar_tensor_tensor(out=hT[:, c, hcol0:hcol0 + G], in0=xT_g[:, c, :],
                                                           scalar=gain[:, c:c + 1], in1=rstd[:],
                                                           op0=ALU.mult, op1=ALU.mult),
             reads=[xT_g, gain, rstd], pwrites=[hT])


def transpose_in_group(K, src, row0, xT_g, xblks, psr, cnt):
    P = K.P
    for tb in range(4):
        xb = xblks.next()
        P.dma("sp", xb[:], src.ap()[row0 + tb * 128: row0 + (tb + 1) * 128, :], reads=[src], writes=[xb])
        for cq in range(4):
            ps = psr.next()
            for c4 in range(4):
                c = cq * 4 + c4
                P.op("pe", lambda e, ps=ps, xb=xb, c=c, c4=c4: e.transpose(ps[:, c4 * 128:(c4 + 1) * 128],
                                                                             xb[:, c * 128:(c + 1) * 128], K.identf[:]),
                     reads=[xb, K.identf], pwrites=[ps])
            eng = "act" if (cnt[0] % 2 == 0) else "dve"
            cnt[0] += 1
            outap = xT_g[:, cq * 4:(cq + 1) * 4, tb * 128:(tb + 1) * 128]
            inap = ps[:].rearrange("p (c t) -> p c t", c=4)
            if eng == "act":
                P.op("act", lambda e, o=outap, i=inap: e.activation(out=o, in_=i, func=AF.Copy), reads=[ps], pwrites=[xT_g])
            else:
                P.op("dve", lambda e, o=outap, i=inap: e.tensor_copy(o, i), reads=[ps], pwrites=[xT_g])


def rope_apply(K, src_ap, src_buf, rows, cos_ap, sin_ap, tabs, perm, out_ap, out_buf, psr, tb_bf, tb_f):
    P = K.P
    qb = tb_bf.next()
    P.op("act", lambda e: e.activation(out=qb[:rows, :], in_=src_ap, func=AF.Copy), reads=[src_buf], writes=[qb])
    ps2 = psr.next()
    mm(P, ps2, ps2[:rows, :], perm[:rows, :rows], qb[:rows, :], True, True, [perm, qb])
    t1 = tb_f.next()
    P.op("dve", lambda e: e.tensor_tensor(out=t1[:rows, :], in0=src_ap, in1=cos_ap, op=ALU.mult),
         reads=[src_buf] + tabs, writes=[t1])
    t2 = tb_f.next()
    P.op("dve", lambda e: e.tensor_tensor(out=t2[:rows, :], in0=ps2[:rows, :], in1=sin_ap, op=ALU.mult),
         reads=[ps2] + tabs, writes=[t2])
    P.op("pool", lambda e: e.tensor_tensor(out=out_ap, in0=t1[:rows, :], in1=t2[:rows, :], op=ALU.add),
         reads=[t1, t2], pwrites=[out_buf])


STG = 2048


def load_w(K, wpool, wd, tile_idx, kc, m):
    wt = wpool.next()
    step = max(1, STG // m)
    k0 = 0
    while k0 < kc:
        k1 = min(kc, k0 + step)
        stg = K.wstage.next()
        n = (k1 - k0) * m
        sv = stg[:, 0:n].rearrange("p (k m) -> p k m", m=m)
        K.P.dma("sp", sv, wd.ap()[tile_idx, :, k0:k1, :], reads=[wd], writes=[stg])
        K.cast_i = getattr(K, "cast_i", 0) + 1
        if K.cast_i % 2 == 0 and getattr(K, "act_cast", True):
            K.P.op("act", lambda e, sv=sv, k0=k0, k1=k1: e.activation(out=wt[:, k0:k1, :m], in_=sv, func=AF.Copy), reads=[stg], pwrites=[wt])
        else:
            K.P.op("pool", lambda e, sv=sv, k0=k0, k1=k1: e.tensor_copy(wt[:, k0:k1, :m], sv), reads=[stg], pwrites=[wt])
        k0 = k1
    return wt


def attention_block(K, q_parts, k_parts, v_fn, den_fn, ntiles, mask_fn, scale, out_ap, out_buf,
                    ps_s, ps_o, ps_d, pbufs, rdbuf, extra_reads):
    P = K.P
    psO = ps_o.next()
    psD = ps_d.next()
    pend = []

    def emit_s(i):
        psS = ps_s.next()
        kp = k_parts(i)
        for j, (ka, qa) in enumerate(zip(kp, q_parts)):
            mm(P, psS, psS[:], ka, qa, j == 0, j == len(kp) - 1, extra_reads)
        pb = pbufs.next()
        P.op("act", lambda e: e.activation(out=pb[:], in_=psS[:], func=AF.Exp, scale=scale), reads=[psS], writes=[pb])
        if mask_fn is not None:
            mk, mkbuf = mask_fn(i)
            P.op("dve", lambda e: e.tensor_tensor(out=pb[:], in0=pb[:], in1=mk, op=ALU.mult), reads=[pb, mkbuf], writes=[pb])
        return pb

    def emit_o(i, pb):
        va, vbufs = v_fn(i)
        mm(P, psO, psO[:], va, pb[:], i == 0, i == ntiles - 1, [pb] + vbufs)
        da, dbufs = den_fn(i)
        mm(P, psD, psD[:], da, pb[:], i == 0, i == ntiles - 1, [pb] + dbufs)

    LAG = 2
    for i in range(ntiles):
        pend.append((i, emit_s(i)))
        if len(pend) > LAG:
            emit_o(*pend.pop(0))
    while pend:
        emit_o(*pend.pop(0))
    rd = rdbuf.next()
    P.op("dve", lambda e: e.reciprocal(rd[:], psD[:]), reads=[psD], writes=[rd])
    P.op("dve", lambda e: e.tensor_tensor(out=out_ap, in0=psO[:], in1=rd[:], op=ALU.mult), reads=[psO, rd], pwrites=[out_buf])


def stage1(K, T):
    P = K.P
    TC, NG, TCX = K.TC, K.NG, K.TCX
    NGX = TCX // G
    st = K.stack
    xT_g = P.sbuf("xT_g", [128, KC, G], F32)
    hT_g = P.sbuf("hT_g", [128, KC, G], BF16)
    sq = P.sbuf("sq", [128, KC, G], BF16)
    rstd = P.sbuf("rstd", [128, G], F32)
    xblks = Rot([P.sbuf("xblk%d" % i, [128, D], F32) for i in range(1)])
    wfm = Rot([P.sbuf("wfm%d" % i, [128, KC, 128], BF16) for i in range(3)])
    wv = Rot([P.sbuf("wv%d" % i, [128, KC, 512], BF16) for i in range(1)])
    K.wstage = Rot([P.sbuf("wstg%d" % i, [128, STG], F32) for i in range(4)])
    csA = Rot([P.sbuf("csA%d" % i, [128, 2, G], F32) for i in range(2)])
    csB = Rot([P.sbuf("csB%d" % i, [128, 2, G], F32) for i in range(2)])
    tb_f = Rot([P.sbuf("tbf%d" % i, [128, G], F32) for i in range(5)])
    tb_bf = Rot([P.sbuf("tbb%d" % i, [128, G], BF16) for i in range(3)])
    kst = Rot([P.sbuf("kst%d" % i, [128, 8, G], BF16) for i in range(1)])
    qst = Rot([P.sbuf("qst%d" % i, [128, 16, G], BF16) for i in range(1)])
    kbst = Rot([P.sbuf("kbst%d" % i, [128, 2, G], BF16) for i in range(1)])
    vst = Rot([P.sbuf("vst%d" % i, [128, 4, 1024], BF16) for i in range(1)])
    vbst = Rot([P.sbuf("vbst%d" % i, [128, 4, 256], BF16) for i in range(1)])
    gpre = K.load_const("l0_mix_pre", [128, KC])
    gq = K.load_const("l0_q_norm", [128, 1])
    gk = K.load_const("l0_k_norm", [128, 1])
    permA = K.load_const("c_permA", [128, 128], BF16)
    permB = K.load_const("c_permB", [128, 128], BF16)
    psr = Rot(K.ps)
    cnt = [int(os.environ.get('S1_CNT0', '0'))]
    x_ext, w_fm, w_vA, w_vB = T["x_ext"], T["w0_fm"], T["w0_vA"], T["w0_vB"]
    cosA, sinA, cosB, sinB = T["cosA"], T["sinA"], T["cosB"], T["sinB"]

    def fm_chunk(tile_idx, evac):
        wt = load_w(K, wfm, w_fm, tile_idx, KC, 128)
        ps = psr.next()
        for kc in range(KC):
            mm(P, ps, ps[:], wt[:, kc, :], hT_g[:, kc, :], kc == 0, kc == KC - 1, [wt, hT_g])
        evac(ps)

    STOP = int(os.environ.get("S1_STOP", "99"))
    for j in range(NGX):
        own = 2 <= j < 2 + NG
        go = j - 2
        GS = os.environ.get("S1_GROUPS")
        if GS and str(j) not in GS.split(","):
            continue
        transpose_in_group(K, x_ext, j * G, xT_g, xblks, psr, cnt)
        if own:
            P.dma("sp", T["xT0"].ap()[:, :, go * G:(go + 1) * G].rearrange("c p t -> p c t"), xT_g[:],
                  reads=[xT_g], pwrites=[T["xT0"]])
        if STOP <= 1:
            continue
        norm_group(K, xT_g, gpre, hT_g, 0, psr, sq, rstd)
        if STOP <= 2:
            continue
        ca = csA.next()
        P.dma("sp", ca[:, 0, :], cosA.ap()[:, j * G:(j + 1) * G], reads=[cosA], pwrites=[ca])
        P.dma("sp", ca[:, 1, :], sinA.ap()[:, j * G:(j + 1) * G], reads=[sinA], pwrites=[ca])
        ks = kst.next()
        for h in range(8):
            fm_chunk(8 + h, lambda ps, h=h: rope_apply(K, ps[:], ps, 128, ca[:, 0, :], ca[:, 1, :], [ca], permA,
                                                       ks[:, h, :], ks, psr, tb_bf, tb_f))
        P.dma("sp", T["kTA"].ap()[:, :, j * G:(j + 1) * G].rearrange("h p t -> p h t"), ks[:], reads=[ks], pwrites=[T["kTA"]])
        if STOP <= 3:
            continue
        vs = vst.next()
        for vt in range(2):
            wt = load_w(K, wv, w_vA, vt, KC, 512)
            for tb in range(4):
                ps = psr.next()
                for kc in range(KC):
                    mm(P, ps, ps[:], hT_g[:, kc, tb * 128:(tb + 1) * 128], wt[:, kc, :], kc == 0, kc == KC - 1, [wt, hT_g])
                P.op("act", lambda e, ps=ps, tb=tb, vt=vt: e.activation(out=vs[:, tb, vt * 512:(vt + 1) * 512], in_=ps[:], func=AF.Copy),
                     reads=[ps], pwrites=[vs])
        P.dma("sp", T["VA"].ap()[j * G:(j + 1) * G, :].rearrange("(tb p) c -> p tb c", p=128), vs[:], reads=[vs], pwrites=[T["VA"]])
        if not own or STOP <= 4:
            continue
        cb = csB.next()
        P.dma("sp", cb[:, 0, :], cosB.ap()[:, go * G:(go + 1) * G], reads=[cosB], pwrites=[cb])
        P.dma("sp", cb[:, 1, :], sinB.ap()[:, go * G:(go + 1) * G], reads=[sinB], pwrites=[cb])
        qs = qst.next()
        for h in range(8):
            fm_chunk(h, lambda ps, h=h: rope_apply(K, ps[:], ps, 128, ca[:, 0, :], ca[:, 1, :], [ca], permA,
                                                   qs[:, h, :], qs, psr, tb_bf, tb_f))

        def b_evac(ps, gain, out_ap, out_buf):
            sqb = tb_bf.next()
            P.op("act", lambda e: e.activation(out=sqb[:], in_=ps[:], func=AF.Square), reads=[ps], writes=[sqb])
            ps3 = psr.next()
            mm(P, ps3, ps3[:], K.ones[:], sqb[:], True, True, [K.ones, sqb])
            r = tb_f.next()
            rstd_from_ss(K, ps3, r, 1.0 / 128)
            qn = tb_f.next()
            P.op("dve", lambda e: e.scalar_tensor_tensor(out=qn[:], in0=ps[:], scalar=gain[:, 0:1], in1=r[:],
                                                          op0=ALU.mult, op1=ALU.mult), reads=[ps, gain, r], writes=[qn])
            rope_apply(K, qn[:], qn, 128, cb[:, 0, :], cb[:, 1, :], [cb], permB, out_ap, out_buf, psr, tb_bf, tb_f)

        for h in range(8):
            fm_chunk(16 + h, lambda ps, h=h: b_evac(ps, gq, qs[:, 8 + h, :], qs))
        P.dma("sp", T["qT0"].ap()[:, :, go * G:(go + 1) * G].rearrange("h p t -> p h t"), qs[:], reads=[qs], pwrites=[T["qT0"]])
        kb = kbst.next()
        for h in range(2):
            fm_chunk(24 + h, lambda ps, h=h: b_evac(ps, gk, kb[:, h, :], kb))
        P.dma("sp", T["kTB"].ap()[:, :, go * G:(go + 1) * G].rearrange("h p t -> p h t"), kb[:], reads=[kb], pwrites=[T["kTB"]])
        vb = vbst.next()
        wt = load_w(K, wv, w_vB, 0, KC, 256)
        for tb in range(4):
            ps = psr.next()
            for kc in range(KC):
                mm(P, ps, ps[:, 0:256], hT_g[:, kc, tb * 128:(tb + 1) * 128], wt[:, kc, 0:256], kc == 0, kc == KC - 1, [wt, hT_g])
            P.op("act", lambda e, ps=ps, tb=tb: e.activation(out=vb[:, tb, :], in_=ps[:, 0:256], func=AF.Copy), reads=[ps], pwrites=[vb])
        P.dma("sp", T["VB"].ap()[go * G:(go + 1) * G, :].rearrange("(tb p) c -> p tb c", p=128), vb[:], reads=[vb], pwrites=[T["VB"]])


def fm_tile(w, c0, m):
    kdim = w.shape[0]
    return np.ascontiguousarray(w[:, c0:c0 + m].reshape(kdim // 128, 128, m).transpose(1, 0, 2))


def vec_tile(v):
    return np.ascontiguousarray(v.reshape(-1, 128).T)


def rope_tables(pos, dim, rows_map):
    inv = (np.float32(10000.0) ** (-np.arange(0, dim, 2, dtype=np.float32) / np.float32(dim))).astype(np.float32)
    ang = pos.astype(np.float32)[None, :] * inv[:, None]
    c = np.cos(ang).astype(np.float32)
    s = np.sin(ang).astype(np.float32)
    idx = np.array([r[0] for r in rows_map])
    sg = np.array([r[1] for r in rows_map], dtype=np.float32)[:, None]
    return np.ascontiguousarray(c[idx]), np.ascontiguousarray(s[idx] * sg)


def perm_matrix(n, half):
    m = np.zeros((n, n), dtype=np.float32)
    for d in range(n):
        w = d % (2 * half)
        m[d + half if w < half else d - half, d] = 1.0
    return m.astype(NPBF)


def dil_mult(o):
    o = np.asarray(o)
    m = (np.abs(o) <= 64).astype(np.float32)
    m += ((o % 4 == 0) & (np.abs(o) <= 256))
    m += ((o % 16 == 0) & (np.abs(o) <= 1024))
    return m


def host_consts(S):
    TC = S // 4
    c = {}
    c["c_ident"] = np.eye(128, dtype=np.float32)
    c["c_permA"] = perm_matrix(128, 64)
    c["c_permB"] = perm_matrix(128, 32)
    c["c_permC"] = perm_matrix(128, 32)
    kk = np.arange(128)[:, None]
    qq = np.arange(512)[None, :]
    c["c_masks"] = np.stack([dil_mult((i - 8) * 128 + kk - qq) for i in range(20)]).astype(NPBF)
    return c


def core_consts(S, core):
    TC = S // 4
    t0 = (core % 4) * TC
    d = {}
    rmA = [(r % 64, -1.0 if r < 64 else 1.0) for r in range(128)]
    pos_ext = np.arange(t0 - HALO, t0 + TC + HALO)
    d["cosA"], d["sinA"] = rope_tables(pos_ext, 128, rmA)
    pos = np.arange(t0, t0 + TC)
    rmB = [((r % 64) % 32, -1.0 if (r % 64) < 32 else 1.0) for r in range(128)]
    cr, sr = rope_tables(pos // 64, 64, rmB)
    cc, sc = rope_tables(pos % 64, 64, rmB)
    d["cosB"] = np.concatenate([cr[:64], cc[64:]], axis=0)
    d["sinB"] = np.concatenate([sr[:64], sc[64:]], axis=0)
    rmC = [(r % 32, -1.0 if r < 32 else 1.0) for r in range(64)]
    d["cosC"], d["sinC"] = rope_tables(pos, 64, rmC)
    nt = (TC + 2 * HALO) // 128
    valid = np.array([1.0 if 0 <= (t0 - HALO + t * 128) < S else 0.0 for t in range(nt)], dtype=np.float32)
    d["denA"] = np.ascontiguousarray(np.broadcast_to(valid[None, :, None], (128, nt, 128))).astype(NPBF)
    return d


def prep_l0_weights(inp):
    w = inp["l0_w_in"]
    cols = [h * 128 for h in range(8)] + [1024 + h * 128 for h in range(8)] + \
           [3072 + h * 128 for h in range(8)] + [4096 + h * 128 for h in range(2)]
    o = {}
    o["w0_fm"] = np.stack([fm_tile(w, c0, 128) for c0 in cols])
    o["w0_vA"] = np.stack([fm_tile(w, 2048 + vt * 512, 512) for vt in range(2)])
    o["w0_vB"] = np.stack([fm_tile(w, 4352, 256)])
    o["l0_mix_pre"] = vec_tile(inp["l0_mix_pre"])
    o["l0_q_norm"] = vec_tile(inp["l0_q_norm"])
    o["l0_k_norm"] = vec_tile(inp["l0_k_norm"])
    return o


def x_ext_for(x, S, core):
    TC = S // 4
    b = core // 4
    t0 = (core % 4) * TC
    out = np.zeros((TC + 2 * HALO, D), dtype=np.float32)
    lo, hi = t0 - HALO, t0 + TC + HALO
    a, e = max(lo, 0), min(hi, S)
    out[a - lo:e - lo] = x[b, a:e]
    return out


def run_launch(K, per_core_inputs, out_names):
    K.P.emit(final_wait=[K.ext_out[n] for n in out_names])
    K.stack.close()
    in_maps = []
    for ci in per_core_inputs:
        in_maps.append({n: np.ascontiguousarray(ci[n]) for n in K.ext_in})
    res = run_bass_kernel_spmd(K.nc, in_maps, core_ids=list(range(8)))
    return [{n: r[n] for n in out_names} for r in res.results]


def build_stage1(S):
    K = KB(S)
    K.common()
    TC, TCX = K.TC, K.TCX
    T = {}
    T["x_ext"] = K.din("x_ext", [TCX, D])
    T["w0_fm"] = K.din("w0_fm", [26, 128, KC, 128])
    T["w0_vA"] = K.din("w0_vA", [2, 128, KC, 512])
    T["w0_vB"] = K.din("w0_vB", [1, 128, KC, 256])
    for n in ("cosA", "sinA"):
        T[n] = K.din(n, [128, TCX])
    for n in ("cosB", "sinB"):
        T[n] = K.din(n, [128, TC])
    T["xT0"] = K.dout("xT0", [KC, 128, TC], F32)
    T["qT0"] = K.dout("qT0", [16, 128, TC], BF16)
    T["kTA"] = K.dout("kTA", [8, 128, TCX], BF16)
    T["VA"] = K.dout("VA", [TCX, 1024], BF16)
    T["kTB"] = K.dout("kTB", [2, 128, TC], BF16)
    T["VB"] = K.dout("VB", [TC, 256], BF16)
    stage1(K, T)
    return K, ["xT0", "qT0", "kTA", "VA", "kTB", "VB"]


class Scope:
    def __init__(self, K):
        self.K = K
        self.st = contextlib.ExitStack()

    def sbuf(self, name, shape, dt):
        return self.K.P.sbuf(name, shape, dt, stack=self.st)

    def close(self):
        self.K.P.barrier()
        self.st.close()


def postnorm_residual(K, sc, y_g, ss, gain, xT_in, xT_out, g, xg, tmp):
    P = K.P
    rstd = tmp["rstd"]
    rstd_from_ss(K, ss, rstd, 1.0 / D)
    P.dma("sp", xg[:], xT_in.ap()[:, :, g * G:(g + 1) * G].rearrange("c p t -> p c t"), reads=[xT_in], writes=[xg])
    for c in range(KC):
        P.op("pool", lambda e, c=c: e.tensor_tensor(out=y_g[:, c, :], in0=y_g[:, c, :], in1=rstd[:], op=ALU.mult),
             reads=[y_g, rstd], writes=[y_g])
        P.op("dve", lambda e, c=c: e.scalar_tensor_tensor(out=xg[:, c, :], in0=y_g[:, c, :], scalar=gain[:, c:c + 1],
                                                           in1=xg[:, c, :], op0=ALU.mult, op1=ALU.add),
             reads=[y_g, gain, xg], writes=[xg])
    if xT_out is not None:
        P.dma("sp", xT_out.ap()[:, :, g * G:(g + 1) * G].rearrange("c p t -> p c t"), xg[:], reads=[xg], pwrites=[xT_out])


def proj_rows_to_y(K, wd, ntile_base, kcn, rhs_fn, rhs_bufs, y_g, wpool, psr, ss, sqr):
    P = K.P
    for c in range(KC):
        wt = load_w(K, wpool, wd, ntile_base + c, kcn, 128)
        ps = psr.next()
        for kc in range(kcn):
            mm(P, ps, ps[:], wt[:, kc, :], rhs_fn(kc), kc == 0, kc == kcn - 1, [wt] + rhs_bufs)
        P.op("act", lambda e, ps=ps, c=c: e.activation(out=y_g[:, c, :], in_=ps[:], func=AF.Copy), reads=[ps], pwrites=[y_g])
        sq = sqr.next()
        P.op("act", lambda e, c=c, sq=sq: e.activation(out=sq[:], in_=y_g[:, c, :], func=AF.Square), reads=[y_g], writes=[sq])
        mm(P, ss, ss[:], K.ones[:], sq[:], c == 0, c == KC - 1, [K.ones, sq])


def mixer_tail(K, T, oT_all, w_out, g_post, g_ffn, xT_in, xT_out, hT_out):
    P = K.P
    sc = Scope(K)
    y_g = sc.sbuf("mt_y", [128, KC, G], F32)
    xg = sc.sbuf("mt_x", [128, KC, G], F32)
    hT_g = sc.sbuf("mt_h", [128, KC, G], BF16)
    sq = sc.sbuf("mt_sq", [128, KC, G], BF16)
    tmp = {"rstd": sc.sbuf("mt_rstd", [128, G], F32)}
    rstd2 = sc.sbuf("mt_rstd2", [128, G], F32)
    sqr = Rot([sc.sbuf("mt_sqc%d" % i, [128, G], BF16) for i in range(2)])
    wpool = Rot([sc.sbuf("mt_w%d" % i, [128, KC, 128], BF16) for i in range(2)])
    K.wstage = Rot([sc.sbuf("mt_ws%d" % i, [128, STG], F32) for i in range(3)])
    psr = Rot(K.ps[0:6])
    for g in range(K.NG):
        ss = K.ps[6]
        proj_rows_to_y(K, w_out, 0, KC, lambda kc, g=g: oT_all[:, kc, g * G:(g + 1) * G], [oT_all], y_g, wpool, psr, ss, sqr)
        postnorm_residual(K, sc, y_g, ss, g_post, xT_in, xT_out, g, xg, tmp)
        norm_group(K, xg, g_ffn, hT_g, 0, Rot([K.ps[7]]), sq, rstd2)
        P.dma("sp", hT_out.ap()[:, :, g * G:(g + 1) * G].rearrange("c p t -> p c t"), hT_g[:], reads=[hT_g], pwrites=[hT_out])
    sc.close()


def stage2(K, T):
    P = K.P
    TC, NG, S, NT = K.TC, K.NG, K.S, K.NT
    scale = 128 ** -0.5
    oT_all = P.sbuf("oT_all", [128, 16, TC], BF16)
    g_post = K.load_const("l0_mix_post", [128, KC])
    g_ffn = K.load_const("l0_ffn_pre", [128, KC])
    sc = Scope(K)
    pbufs = Rot([sc.sbuf("pb%d" % i, [128, G], BF16) for i in range(4)])
    rdbuf = Rot([sc.sbuf("rd%d" % i, [128, G], F32) for i in range(2)])
    ps_s, ps_o, ps_d = Rot(K.ps[0:3]), Rot(K.ps[3:5]), Rot(K.ps[5:7])
    qg = sc.sbuf("qg", [128, 16, G], BF16)
    scA = Scope(K)
    masks = scA.sbuf("masks_sb", [128, 20, G], BF16)
    P.dma("sp", masks[:], T["c_masks"].ap().rearrange("i p q -> p i q"), reads=[T["c_masks"]], writes=[masks])
    denA = scA.sbuf("denA_sb", [128, K.TCX // 128, 128], BF16)
    P.dma("sp", denA[:], T["denA"].ap(), reads=[T["denA"]], writes=[denA])
    kAs = Rot([scA.sbuf("kA%d" % i, [128, 2560], BF16) for i in range(2)])
    vAs = Rot([scA.sbuf("vA%d" % i, [128, 20, 128], BF16) for i in range(2)])
    for g in range(NG):
        P.dma("sp", qg[:], T["qT0"].ap()[:, :, g * G:(g + 1) * G].rearrange("h p t -> p h t"), reads=[T["qT0"]], writes=[qg])
        for h in range(8):
            kA = kAs.next()
            vA = vAs.next()
            P.dma("sp", kA[:], T["kTA"].ap()[h, :, g * G:g * G + 2560], reads=[T["kTA"]], writes=[kA])
            P.dma("sp", vA[:], T["VA"].ap()[g * G:g * G + 2560, h * 128:(h + 1) * 128].rearrange("(t p) c -> p t c", p=128),
                  reads=[T["VA"]], writes=[vA])
            attention_block(K, [qg[:, h, :]], lambda i, kA=kA: [kA[:, i * 128:(i + 1) * 128]],
                            lambda i, vA=vA: (vA[:, i, :], [vA]),
                            lambda i, g=g: (denA[:, g * 4 + i, :], [denA]),
                            20, lambda i: (masks[:, i, :], masks), scale,
                            oT_all[:, h, g * G:(g + 1) * G], oT_all, ps_s, ps_o, ps_d, pbufs, rdbuf, [kA, qg])
    scA.close()
    scB = Scope(K)
    kB = scB.sbuf("kB", [128, 2, S], BF16)
    vB = scB.sbuf("vB", [128, NT, 256], BF16)
    ntr = TC // 128
    for r in range(4):
        P.dma("sp", kB[:, :, r * TC:(r + 1) * TC], T["kTB_all"].ap()[r].rearrange("h p t -> p h t"), reads=[T["kTB_all"]], pwrites=[kB])
        P.dma("sp", vB[:, r * ntr:(r + 1) * ntr, :], T["VB_all"].ap()[r].rearrange("(t p) c -> p t c", p=128), reads=[T["VB_all"]], pwrites=[vB])
    for g in range(NG):
        P.dma("sp", qg[:], T["qT0"].ap()[:, :, g * G:(g + 1) * G].rearrange("h p t -> p h t"), reads=[T["qT0"]], writes=[qg])
        for h in range(8):
            kv = h // 4
            attention_block(K, [qg[:, 8 + h, :]], lambda i, kv=kv: [kB[:, kv, i * 128:(i + 1) * 128]],
                            lambda i, kv=kv: (vB[:, i, kv * 128:(kv + 1) * 128], [vB]),
                            lambda i: (K.ones[:], [K.ones]),
                            NT, None, scale,
                            oT_all[:, 8 + h, g * G:(g + 1) * G], oT_all, ps_s, ps_o, ps_d, pbufs, rdbuf, [kB, qg])
    scB.close()
    sc.close()
    mixer_tail(K, T, oT_all, T["w0_out"], g_post, g_ffn, T["xT0"], T["xT1"], T["hT1"])


def build_stage2(S):
    K = KB(S)
    K.common()
    TC, TCX = K.TC, K.TCX
    T = {}
    T["qT0"] = K.din("qT0", [16, 128, TC], BF16)
    T["kTA"] = K.din("kTA", [8, 128, TCX], BF16)
    T["VA"] = K.din("VA", [TCX, 1024], BF16)
    T["kTB_all"] = K.din("kTB_all", [4, 2, 128, TC], BF16)
    T["VB_all"] = K.din("VB_all", [4, TC, 256], BF16)
    T["c_masks"] = K.din("c_masks", [20, 128, G], BF16)
    T["denA"] = K.din("denA", [128, TCX // 128, 128], BF16)
    T["w0_out"] = K.din("w0_out", [16, 128, KC, 128])
    T["xT0"] = K.din("xT0", [KC, 128, TC], F32)
    T["xT1"] = K.dout("xT1", [KC, 128, TC], F32)
    T["hT1"] = K.dout("hT1", [KC, 128, TC], BF16)
    stage2(K, T)
    return K, ["xT1", "hT1"]


def prep_out_weights(inp, name, key):
    w = inp[name]
    return {key: np.stack([fm_tile(w, c * 128, 128) for c in range(16)])}


def ffn_pass(K, T, L, xT_in, xT_out, y_out):
    P = K.P
    NG, TC = K.NG, K.TC
    cw = K.load_const("l%d_cw" % L, [128, NFF, 3])
    cb = K.load_const("l%d_cb" % L, [128, NFF])
    g_post = K.load_const("l%d_ffn_post" % L, [128, KC])
    sc = Scope(K)
    hTx = sc.sbuf("f_hTx", [128, KC, G + 2], BF16)
    hs = sc.sbuf("f_hs", [128, KC, 2], BF16)
    aT = sc.sbuf("f_aT", [128, NFF, G], BF16)
    y_g = sc.sbuf("f_y", [128, KC, G], F32)
    xg = sc.sbuf("f_x", [128, KC, G], F32)
    tmp = {"rstd": sc.sbuf("f_rstd", [128, G], F32)}
    wup = Rot([sc.sbuf("f_wup%d" % i, [128, KC, 256], BF16) for i in range(2)])
    wdn = Rot([sc.sbuf("f_wdn%d" % i, [128, NFF, 128], BF16) for i in range(1)])
    K.wstage = Rot([sc.sbuf("f_ws%d" % i, [128, STG], F32) for i in range(4)])
    gsbs = Rot([sc.sbuf("f_gsb%d" % i, [128, G + 2], F32) for i in range(2)])
    accs = Rot([sc.sbuf("f_acc%d" % i, [128, G], F32) for i in range(2)])
    gls = Rot([sc.sbuf("f_gl%d" % i, [128, G], F32) for i in range(2)])
    sqr = Rot([sc.sbuf("f_sqc%d" % i, [128, G], BF16) for i in range(2)])
    hT, halo, w_up, w_dn = T["hT_ffn"], T["h_halo"], T["w_up"], T["w_dn"]
    P.dma("sp", hs[:], halo.ap(), reads=[halo], writes=[hs])
    psG, psV, psH = Rot(K.ps[0:2]), Rot(K.ps[2:4]), Rot([K.ps[4]])
    psr = Rot(K.ps[4:6])
    if y_out is not None:
        yblks = Rot([sc.sbuf("f_yb%d" % i, [128, D], F32) for i in range(1)])
    for g in range(NG):
        lo = g * G - 1
        hi = g * G + G + 1
        slo, shi = max(lo, 0), min(hi, TC)
        P.dma("sp", hTx[:, :, slo - lo:shi - lo], hT.ap()[:, :, slo:shi].rearrange("c p t -> p c t"), reads=[hT], pwrites=[hTx])
        if g == 0:
            P.op("dve", lambda e: e.tensor_copy(hTx[:, :, 0:1], hs[:, :, 0:1]), reads=[hs], pwrites=[hTx])
        if g == NG - 1:
            P.op("dve", lambda e: e.tensor_copy(hTx[:, :, G + 1:G + 2], hs[:, :, 1:2]), reads=[hs], pwrites=[hTx])
        for j in range(NFF):
            wt = load_w(K, wup, w_up, j, KC, 256)
            pg, pv, ph = psG.next(), psV.next(), psH.next()
            for kc in range(KC):
                mm(P, pg, pg[:], wt[:, kc, 0:128], hTx[:, kc, 1:G + 1], kc == 0, kc == KC - 1, [wt, hTx])
            for kc in range(KC):
                mm(P, ph, ph[:, 0:1], wt[:, kc, 0:128], hTx[:, kc, 0:1], kc == 0, kc == KC - 1, [wt, hTx])
            for kc in range(KC):
                mm(P, ph, ph[:, 2:3], wt[:, kc, 0:128], hTx[:, kc, G + 1:G + 2], kc == 0, kc == KC - 1, [wt, hTx])
            for kc in range(KC):
                mm(P, pv, pv[:], wt[:, kc, 128:256], hTx[:, kc, 1:G + 1], kc == 0, kc == KC - 1, [wt, hTx])
            gsb, acc, gl = gsbs.next(), accs.next(), gls.next()
            P.op("act", lambda e, gsb=gsb, pg=pg: e.activation(out=gsb[:, 1:G + 1], in_=pg[:], func=AF.Copy), reads=[pg], pwrites=[gsb])
            P.op("dve", lambda e, gsb=gsb, ph=ph: e.tensor_copy(gsb[:, 0:1], ph[:, 0:1]), reads=[ph], pwrites=[gsb])
            P.op("dve", lambda e, gsb=gsb, ph=ph: e.tensor_copy(gsb[:, G + 1:G + 2], ph[:, 2:3]), reads=[ph], pwrites=[gsb])
            P.op("act", lambda e, gsb=gsb, acc=acc, j=j: e.activation(out=acc[:], in_=gsb[:, 1:G + 1], func=AF.Identity,
                                                                      bias=cb[:, j:j + 1], scale=cw[:, j, 1:2]),
                 reads=[gsb, cb, cw], writes=[acc])
            P.op("dve", lambda e, gsb=gsb, acc=acc, j=j: e.scalar_tensor_tensor(out=acc[:], in0=gsb[:, 0:G], scalar=cw[:, j, 0:1], in1=acc[:],
                                                                               op0=ALU.mult, op1=ALU.add), reads=[gsb, cw, acc], writes=[acc])
            P.op("dve", lambda e, gsb=gsb, acc=acc, j=j: e.scalar_tensor_tensor(out=acc[:], in0=gsb[:, 2:G + 2], scalar=cw[:, j, 2:3], in1=acc[:],
                                                                               op0=ALU.mult, op1=ALU.add), reads=[gsb, cw, acc], writes=[acc])
            P.op("act", lambda e, acc=acc, gl=gl: e.activation(out=gl[:], in_=acc[:], func=AF.Gelu), reads=[acc], writes=[gl])
            P.op("dve", lambda e, gl=gl, pv=pv, j=j: e.tensor_tensor(out=aT[:, j, :], in0=gl[:], in1=pv[:], op=ALU.mult),
                 reads=[gl, pv], pwrites=[aT])
        ss = K.ps[6]
        proj_rows_to_y(K, w_dn, 0, NFF, lambda kc: aT[:, kc, :], [aT], y_g, wdn, psr, ss, sqr)
        postnorm_residual(K, sc, y_g, ss, g_post, xT_in, xT_out, g, xg, tmp)
        if y_out is not None:
            pst = Rot([K.ps[7], K.ps[5]])
            for tb in range(4):
                yb = yblks.next()
                for cq in range(4):
                    ps = pst.next()
                    for c4 in range(4):
                        c = cq * 4 + c4
                        P.op("pe", lambda e, ps=ps, c=c, c4=c4, tb=tb: e.transpose(ps[:, c4 * 128:(c4 + 1) * 128],
                                                                                    xg[:, c, tb * 128:(tb + 1) * 128], K.identf[:]),
                             reads=[xg, K.identf], pwrites=[ps])
                    P.op("dve", lambda e, ps=ps, yb=yb, cq=cq: e.tensor_copy(yb[:, cq * 512:(cq + 1) * 512], ps[:]), reads=[ps], pwrites=[yb])
                P.dma("sp", y_out.ap()[g * G + tb * 128:g * G + (tb + 1) * 128, :], yb[:], reads=[yb], pwrites=[y_out])
    sc.close()


def l1_pre(K, T):
    P = K.P
    NG, TC = K.NG, K.TC
    gpre = K.load_const("l1_mix_pre", [128, KC])
    gqa = K.load_const("l1_q_a_norm", [128, 4])
    gkva = K.load_const("l1_kv_a_norm", [128, 4])
    permC = K.load_const("c_permC", [128, 128], BF16)
    sc = Scope(K)
    xg = sc.sbuf("p_x", [128, KC, G], F32)
    hT_g = sc.sbuf("p_h", [128, KC, G], BF16)
    sq = sc.sbuf("p_sq", [128, KC, G], BF16)
    rstd = sc.sbuf("p_rstd", [128, G], F32)
    cq = sc.sbuf("p_cq", [128, 8, G], F32)
    cn = sc.sbuf("p_cn", [128, 8, G], BF16)
    wfm = Rot([sc.sbuf("p_w%d" % i, [128, KC, 128], BF16) for i in range(1)])
    wq = Rot([sc.sbuf("p_wq%d" % i, [128, 4, 192], BF16) for i in range(2)])
    wv = Rot([sc.sbuf("p_wv%d" % i, [128, 4, 512], BF16) for i in range(2)])
    K.wstage = Rot([sc.sbuf("p_ws%d" % i, [128, STG], F32) for i in range(2)])
    tb_f = Rot([sc.sbuf("p_tf%d" % i, [128, G], F32) for i in range(3)])
    tb_bf = Rot([sc.sbuf("p_tb%d" % i, [128, G], BF16) for i in range(3)])
    sqr = Rot([sc.sbuf("p_sqc%d" % i, [128, G], BF16) for i in range(2)])
    rq = sc.sbuf("p_rq", [128, G], F32)
    rkv = sc.sbuf("p_rkv", [128, G], F32)
    qst = sc.sbuf("p_qst", [128, 16, G], BF16)
    qrst = sc.sbuf("p_qrst", [64, 16, G], BF16)
    kst = sc.sbuf("p_kst", [128, 16, G], BF16)
    krst = sc.sbuf("p_krst", [64, G], BF16)
    vsts = Rot([sc.sbuf("p_vst%d" % i, [128, 4, 512], BF16) for i in range(2)])
    cs = Rot([sc.sbuf("p_cs%d" % i, [64, 2, G], F32) for i in range(2)])
    psr = Rot(K.ps[0:6])
    xT2 = T["xT2"]
    for g in range(NG):
        P.dma("sp", xg[:], xT2.ap()[:, :, g * G:(g + 1) * G].rearrange("c p t -> p c t"), reads=[xT2], writes=[xg])
        norm_group(K, xg, gpre, hT_g, 0, Rot([K.ps[7]]), sq, rstd)
        c_ = cs.next()
        P.dma("sp", c_[:, 0, :], T["cosC"].ap()[:, g * G:(g + 1) * G], reads=[T["cosC"]], pwrites=[c_])
        P.dma("sp", c_[:, 1, :], T["sinC"].ap()[:, g * G:(g + 1) * G], reads=[T["sinC"]], pwrites=[c_])
        ssq, sskv = K.ps[6], K.ps[7]
        for t in range(9):
            wt = load_w(K, wfm, T["w1_in"], t, KC, 128)
            ps = psr.next()
            for kc in range(KC):
                mm(P, ps, ps[:], wt[:, kc, :], hT_g[:, kc, :], kc == 0, kc == KC - 1, [wt, hT_g])
            if t < 8:
                P.op("act", lambda e, ps=ps, t=t: e.activation(out=cq[:, t, :], in_=ps[:], func=AF.Copy), reads=[ps], pwrites=[cq])
                s_ = sqr.next()
                P.op("act", lambda e, t=t, s_=s_: e.activation(out=s_[:], in_=cq[:, t, :], func=AF.Square), reads=[cq], writes=[s_])
                acc = ssq if t < 4 else sskv
                mm(P, acc, acc[:], K.ones[:], s_[:], t % 4 == 0, t % 4 == 3, [K.ones, s_])
            else:
                rope_apply(K, ps[:64, :], ps, 64, c_[:, 0, :], c_[:, 1, :], [c_], permC, krst[:, :], krst, psr, tb_bf, tb_f)
                P.dma("sp", T["krT"].ap()[:, g * G:(g + 1) * G], krst[:], reads=[krst], pwrites=[T["krT"]])
        rstd_from_ss(K, ssq, rq, 1.0 / 512)
        rstd_from_ss(K, sskv, rkv, 1.0 / 512)
        for t in range(8):
            gg, rr = (gqa, rq) if t < 4 else (gkva, rkv)
            P.op("dve", lambda e, t=t, gg=gg, rr=rr: e.scalar_tensor_tensor(out=cn[:, t, :], in0=cq[:, t, :], scalar=gg[:, t % 4:t % 4 + 1],
                                                                            in1=rr[:], op0=ALU.mult, op1=ALU.mult),
                 reads=[cq, gg, rr], pwrites=[cn])
        for h in range(16):
            wt = load_w(K, wq, T["w1_uq"], h, 4, 192)
            ps = psr.next()
            for kc in range(4):
                mm(P, ps, ps[:], wt[:, kc, 0:128], cn[:, kc, :], kc == 0, kc == 3, [wt, cn])
            P.op("act", lambda e, ps=ps, h=h: e.activation(out=qst[:, h, :], in_=ps[:], func=AF.Copy), reads=[ps], pwrites=[qst])
            ps = psr.next()
            for kc in range(4):
                mm(P, ps, ps[:64, :], wt[:, kc, 128:192], cn[:, kc, :], kc == 0, kc == 3, [wt, cn])
            rope_apply(K, ps[:64, :], ps, 64, c_[:, 0, :], c_[:, 1, :], [c_], permC, qrst[:, h, :], qrst, psr, tb_bf, tb_f)
        P.dma("sp", T["qT1"].ap()[:, :, g * G:(g + 1) * G].rearrange("h p t -> p h t"), qst[:], reads=[qst], pwrites=[T["qT1"]])
        P.dma("sp", T["qrT1"].ap()[:, :, g * G:(g + 1) * G].rearrange("h p t -> p h t"), qrst[:], reads=[qrst], pwrites=[T["qrT1"]])
        for h in range(16):
            wt = load_w(K, wfm, T["w1_uk"], h, 4, 128)
            ps = psr.next()
            for kc in range(4):
                mm(P, ps, ps[:], wt[:, kc, :], cn[:, 4 + kc, :], kc == 0, kc == 3, [wt, cn])
            P.op("act", lambda e, ps=ps, h=h: e.activation(out=kst[:, h, :], in_=ps[:], func=AF.Copy), reads=[ps], pwrites=[kst])
        P.dma("sp", T["kTC"].ap()[:, :, g * G:(g + 1) * G].rearrange("h p t -> p h t"), kst[:], reads=[kst], pwrites=[T["kTC"]])
        for vt in range(4):
            wt = load_w(K, wv, T["w1_uv"], vt, 4, 512)
            vst = vsts.next()
            for tb in range(4):
                ps = psr.next()
                for kc in range(4):
                    mm(P, ps, ps[:], cn[:, 4 + kc, tb * 128:(tb + 1) * 128], wt[:, kc, :], kc == 0, kc == 3, [wt, cn])
                P.op("act", lambda e, ps=ps, tb=tb, vst=vst: e.activation(out=vst[:, tb, :], in_=ps[:], func=AF.Copy),
                     reads=[ps], pwrites=[vst])
            P.dma("sp", T["VC"].ap()[g * G:(g + 1) * G, vt * 512:(vt + 1) * 512].rearrange("(tb p) c -> p tb c", p=128), vst[:],
                  reads=[vst], pwrites=[T["VC"]])
    sc.close()


def stage4(K, T):
    P = K.P
    TC, NG, S, NT = K.TC, K.NG, K.S, K.NT
    scale = 192 ** -0.5
    oT_all = P.sbuf("oT_all", [128, 16, TC], BF16)
    g_post = K.load_const("l1_mix_post", [128, KC])
    g_ffn = K.load_const("l1_ffn_pre", [128, KC])
    sc = Scope(K)
    pbufs = Rot([sc.sbuf("pb%d" % i, [128, G], BF16) for i in range(4)])
    rdbuf = Rot([sc.sbuf("rd%d" % i, [128, G], F32) for i in range(2)])
    ps_s, ps_o, ps_d = Rot(K.ps[0:3]), Rot(K.ps[3:5]), Rot(K.ps[5:7])
    kr = sc.sbuf("c_kr", [64, S], BF16)
    khs = Rot([sc.sbuf("c_kh%d" % i, [128, S], BF16) for i in range(2)])
    vhs = Rot([sc.sbuf("c_vh%d" % i, [128, NT, 128], BF16) for i in range(2)])
    qhs = Rot([sc.sbuf("c_qh%d" % i, [128, TC], BF16) for i in range(2)])
    qrs = Rot([sc.sbuf("c_qr%d" % i, [64, TC], BF16) for i in range(2)])
    ntr = TC // 128
    for r in range(4):
        P.dma("sp", kr[:, r * TC:(r + 1) * TC], T["krT_all"].ap()[r], reads=[T["krT_all"]], pwrites=[kr])
    for h in range(16):
        kh, vh, qh, qr = khs.next(), vhs.next(), qhs.next(), qrs.next()
        for r in range(4):
            P.dma("sp", kh[:, r * TC:(r + 1) * TC], T["kTC_all"].ap()[r, h], reads=[T["kTC_all"]], pwrites=[kh])
            P.dma("sp", vh[:, r * ntr:(r + 1) * ntr, :], T["VC_all"].ap()[r, :, h * 128:(h + 1) * 128].rearrange("(t p) c -> p t c", p=128),
                  reads=[T["VC_all"]], pwrites=[vh])
        P.dma("sp", qh[:], T["qT1"].ap()[h], reads=[T["qT1"]], writes=[qh])
        P.dma("sp", qr[:], T["qrT1"].ap()[h], reads=[T["qrT1"]], writes=[qr])
        for g in range(NG):
            attention_block(K, [qh[:, g * G:(g + 1) * G], qr[:, g * G:(g + 1) * G]],
                            lambda i, kh=kh: [kh[:, i * 128:(i + 1) * 128], kr[:, i * 128:(i + 1) * 128]],
                            lambda i, vh=vh: (vh[:, i, :], [vh]),
                            lambda i: (K.ones[:], [K.ones]),
                            NT, None, scale,
                            oT_all[:, h, g * G:(g + 1) * G], oT_all, ps_s, ps_o, ps_d, pbufs, rdbuf, [kh, kr, qh, qr])
    sc.close()
    mixer_tail(K, T, oT_all, T["w1_out"], g_post, g_ffn, T["xT2"], T["xT3"], T["hT3"])


def build_stage3(S):
    K = KB(S)
    K.common()
    TC = K.TC
    T = {}
    T["hT_ffn"] = K.din("hT_ffn", [KC, 128, TC], BF16)
    T["h_halo"] = K.din("h_halo", [128, KC, 2], BF16)
    T["w_up"] = K.din("w_up", [NFF, 128, KC, 256])
    T["w_dn"] = K.din("w_dn", [16, 128, NFF, 128])
    xT1 = K.din("xT_in", [KC, 128, TC], F32)
    T["xT2"] = K.dout("xT2", [KC, 128, TC], F32)
    ffn_pass(K, T, 0, xT1, T["xT2"], None)
    T["w1_in"] = K.din("w1_in", [9, 128, KC, 128])
    T["w1_uq"] = K.din("w1_uq", [16, 128, 4, 192])
    T["w1_uk"] = K.din("w1_uk", [16, 128, 4, 128])
    T["w1_uv"] = K.din("w1_uv", [4, 128, 4, 512])
    T["cosC"] = K.din("cosC", [64, TC])
    T["sinC"] = K.din("sinC", [64, TC])
    T["qT1"] = K.dout("qT1", [16, 128, TC], BF16)
    T["qrT1"] = K.dout("qrT1", [16, 64, TC], BF16)
    T["kTC"] = K.dout("kTC", [16, 128, TC], BF16)
    T["krT"] = K.dout("krT", [64, TC], BF16)
    T["VC"] = K.dout("VC", [TC, 2048], BF16)
    l1_pre(K, T)
    return K, ["xT2", "qT1", "qrT1", "kTC", "krT", "VC"]


def build_stage4(S):
    K = KB(S)
    K.common()
    TC = K.TC
    T = {}
    T["qT1"] = K.din("qT1", [16, 128, TC], BF16)
    T["qrT1"] = K.din("qrT1", [16, 64, TC], BF16)
    T["kTC_all"] = K.din("kTC_all", [4, 16, 128, TC], BF16)
    T["krT_all"] = K.din("krT_all", [4, 64, TC], BF16)
    T["VC_all"] = K.din("VC_all", [4, TC, 2048], BF16)
    T["w1_out"] = K.din("w1_out", [16, 128, KC, 128])
    T["xT2"] = K.din("xT2", [KC, 128, TC], F32)
    T["xT3"] = K.dout("xT3", [KC, 128, TC], F32)
    T["hT3"] = K.dout("hT3", [KC, 128, TC], BF16)
    stage4(K, T)
    return K, ["xT3", "hT3"]


def build_stage5(S):
    K = KB(S)
    K.common()
    TC = K.TC
    T = {}
    T["hT_ffn"] = K.din("hT_ffn", [KC, 128, TC], BF16)
    T["h_halo"] = K.din("h_halo", [128, KC, 2], BF16)
    T["w_up"] = K.din("w_up", [NFF, 128, KC, 256])
    T["w_dn"] = K.din("w_dn", [16, 128, NFF, 128])
    xT3 = K.din("xT_in", [KC, 128, TC], F32)
    xT4 = K.dout("xT4", [KC, 128, TC], F32)
    ffn_pass(K, T, 1, xT3, xT4, None)
    return K, ["xT4"]


def prep_ffn_weights(inp, L):
    p = "l%d_" % L
    if L == 0:
        wu, wd, cw_, cb_, fp_ = inp["l0_w_up"], inp["l0_w_down"], inp["l0_conv_w"], inp["l0_conv_b"], inp["l0_ffn_post"]
    else:
        wu, wd, cw_, cb_, fp_ = inp["l1_w_up"], inp["l1_w_down"], inp["l1_conv_w"], inp["l1_conv_b"], inp["l1_ffn_post"]
    o = {}
    o["w_up"] = np.stack([np.concatenate([fm_tile(wu, j * 128, 128), fm_tile(wu, DFF + j * 128, 128)], axis=2) for j in range(NFF)])
    o["w_dn"] = np.stack([fm_tile(wd, c * 128, 128) for c in range(16)])
    o[p + "cw"] = np.ascontiguousarray(cw_.reshape(3, NFF, 128).transpose(2, 1, 0))
    o[p + "cb"] = vec_tile(cb_)
    o[p + "ffn_post"] = vec_tile(fp_)
    return o


def prep_l1_weights(inp):
    o = {}
    w = inp["l1_w_in"]
    wpad = np.zeros((D, 9 * 128), dtype=np.float32)
    wpad[:, :1088] = w
    o["w1_in"] = np.stack([fm_tile(wpad, t * 128, 128) for t in range(9)])
    o["w1_uq"] = np.stack([fm_tile(inp["l1_w_uq"], h * 192, 192) for h in range(16)])
    wkv = inp["l1_w_ukv"]
    o["w1_uk"] = np.stack([fm_tile(wkv, h * 256, 128) for h in range(16)])
    vcols = np.concatenate([np.arange(h * 256 + 128, h * 256 + 256) for h in range(16)])
    wv = np.ascontiguousarray(wkv[:, vcols])
    o["w1_uv"] = np.stack([fm_tile(wv, vt * 512, 512) for vt in range(4)])
    o["l1_mix_pre"] = vec_tile(inp["l1_mix_pre"])
    o["l1_q_a_norm"] = vec_tile(inp["l1_q_a_norm"])
    o["l1_kv_a_norm"] = vec_tile(inp["l1_kv_a_norm"])
    return o


def halo_cols(hT_list, S):
    out = []
    for c in range(8):
        h = np.zeros((128, KC, 2), dtype=NPBF)
        if c % 4 != 0:
            h[:, :, 0] = hT_list[c - 1][:, :, -1].T
        if c % 4 != 3:
            h[:, :, 1] = hT_list[c + 1][:, :, 0].T
        out.append(h)
    return out


def kernel(**inputs):
    inp = {k: np.asarray(v) for k, v in inputs.items()}
    x = inp["x"]
    B, S, _ = x.shape
    TC = S // 4
    cst = host_consts(S)
    cc = [core_consts(S, c) for c in range(8)]
    K, outs = build_stage1(S)
    w0 = prep_l0_weights(inp)
    pc = []
    for c in range(8):
        d = dict(cst); d.update(w0); d.update(cc[c]); d["x_ext"] = x_ext_for(x, S, c)
        pc.append(d)
    r1 = run_launch(K, pc, outs)
    del w0
    K, outs = build_stage2(S)
    wo = prep_out_weights(inp, "l0_w_out", "w0_out")
    wo["l0_mix_post"] = vec_tile(inp["l0_mix_post"]); wo["l0_ffn_pre"] = vec_tile(inp["l0_ffn_pre"])
    pc = []
    for c in range(8):
        b = c // 4
        d = dict(cst); d.update(wo); d.update(cc[c])
        for n in ("qT0", "kTA", "VA", "xT0"):
            d[n] = r1[c][n]
        d["kTB_all"] = np.stack([r1[b * 4 + r]["kTB"] for r in range(4)])
        d["VB_all"] = np.stack([r1[b * 4 + r]["VB"] for r in range(4)])
        pc.append(d)
    r2 = run_launch(K, pc, outs)
    del r1
    K, outs = build_stage3(S)
    wf = prep_ffn_weights(inp, 0)
    wf.update(prep_l1_weights(inp))
    hal = halo_cols([r["hT1"] for r in r2], S)
    pc = []
    for c in range(8):
        d = dict(cst); d.update(wf); d.update(cc[c])
        d["hT_ffn"] = r2[c]["hT1"]; d["h_halo"] = hal[c]; d["xT_in"] = r2[c]["xT1"]
        pc.append(d)
    r3 = run_launch(K, pc, outs)
    del r2, wf
    K, outs = build_stage4(S)
    wo = prep_out_weights(inp, "l1_w_out", "w1_out")
    wo["l1_mix_post"] = vec_tile(inp["l1_mix_post"]); wo["l1_ffn_pre"] = vec_tile(inp["l1_ffn_pre"])
    pc = []
    for c in range(8):
        b = c // 4
        d = dict(cst); d.update(wo); d.update(cc[c])
        for n in ("qT1", "qrT1", "xT2"):
            d[n] = r3[c][n]
        d["kTC_all"] = np.stack([r3[b * 4 + r]["kTC"] for r in range(4)])
        d["krT_all"] = np.stack([r3[b * 4 + r]["krT"] for r in range(4)])
        d["VC_all"] = np.stack([r3[b * 4 + r]["VC"] for r in range(4)])
        pc.append(d)
    r4 = run_launch(K, pc, outs)
    del r3
    K, outs = build_stage5(S)
    wf = prep_ffn_weights(inp, 1)
    hal = halo_cols([r["hT3"] for r in r4], S)
    pc = []
    for c in range(8):
        d = dict(cst); d.update(wf); d.update(cc[c])
        d["hT_ffn"] = r4[c]["hT3"]; d["h_halo"] = hal[c]; d["xT_in"] = r4[c]["xT3"]
        pc.append(d)
    r5 = run_launch(K, pc, outs)
    out = np.zeros((B, S, D), dtype=np.float32)
    for c in range(8):
        out[c // 4, (c % 4) * TC:(c % 4 + 1) * TC] = r5[c]["xT4"].transpose(2, 0, 1).reshape(TC, D)
    return out
```
